# Optimizing a Trainium2 kernel written in Bass

```python
import math
import jax, jax.numpy as jnp
from jax import lax
import numpy as np

D_MODEL = 2048
BATCH = 2
SEQ = 4096
DEPTH = 2

CHUNK = 64
N_META = 16
Q_BLOCK = 128
N_MIXERS = 2
RMS_EPS = 1e-6

DA_HEADS = 8
DA_HEAD_DIM = D_MODEL // (2 * DA_HEADS)
DA_V_DIM = 2 * DA_HEAD_DIM
ROPE_THETA = 500000.0
ROPE_DIM = DA_HEAD_DIM // 4

POOL_WINDOWS = (2, 4, 8, 16)
POOL_GROUPS = len(POOL_WINDOWS)
POOL_GROUP_DIM = D_MODEL // POOL_GROUPS

N_ATTN_LAYERS = (DEPTH + 1) // 2
N_POOL_LAYERS = DEPTH // 2

kernel_name = "hybrid_diffattn_pool_streaming_trunk"


def rms_norm(x, g):
    xf = x.astype(jnp.float32)
    y = xf * lax.rsqrt(jnp.mean(xf * xf, axis=-1, keepdims=True) + RMS_EPS)
    return (y * g.astype(jnp.float32)).astype(x.dtype)


def chunk_ids(length):
    p = jnp.arange(length)
    return jnp.where(p < N_META, 0, (p - N_META) // CHUNK + 1)


def rope_tables(length):
    pos = jnp.arange(length, dtype=jnp.float32)
    inv = ROPE_THETA ** (-jnp.arange(0, ROPE_DIM, 2, dtype=jnp.float32) / ROPE_DIM)
    ang = pos[:, None] * inv[None, :]
    return jnp.cos(ang), jnp.sin(ang)


def apply_partial_rope(t, cos, sin):
    half = ROPE_DIM // 2
    c = cos.astype(t.dtype)
    s = sin.astype(t.dtype)
    r1 = t[..., :half]
    r2 = t[..., half:ROPE_DIM]
    return jnp.concatenate([r1 * c - r2 * s, r2 * c + r1 * s, t[..., ROPE_DIM:]], axis=-1)


def diff_attend(q1, q2, k1, k2, v, qc, kc, lam):
    scale = DA_HEAD_DIM ** -0.5
    mask = kc[None, :] <= qc[:, None]
    neg = jnp.finfo(jnp.float32).min
    s1 = jnp.einsum('bhqd,bhkd->bhqk', q1, k1).astype(jnp.float32) * scale
    s2 = jnp.einsum('bhqd,bhkd->bhqk', q2, k2).astype(jnp.float32) * scale
    p1 = jax.nn.softmax(jnp.where(mask, s1, neg), axis=-1)
    p2 = jax.nn.softmax(jnp.where(mask, s2, neg), axis=-1)
    a = p1 - lam * p2
    return jnp.einsum('bhqk,bhkd->bhqd', a.astype(v.dtype), v)


def diff_attention_mixer(h, w_in, w_out, lq1, lk1, lq2, lk2, subln_g, layer_idx, cos, sin):
    B, L, D = h.shape
    proj = h @ w_in
    q, k, v, g = jnp.split(proj, 4, axis=-1)
    q = q.reshape(B, L, DA_HEADS, 2, DA_HEAD_DIM).transpose(0, 2, 3, 1, 4)
    k = k.reshape(B, L, DA_HEADS, 2, DA_HEAD_DIM).transpose(0, 2, 3, 1, 4)
    v = v.reshape(B, L, DA_HEADS, DA_V_DIM).transpose(0, 2, 1, 3)
    q1 = apply_partial_rope(q[:, :, 0], cos, sin)
    q2 = apply_partial_rope(q[:, :, 1], cos, sin)
    k1 = apply_partial_rope(k[:, :, 0], cos, sin)
    k2 = apply_partial_rope(k[:, :, 1], cos, sin)

    lam_init = 0.8 - 0.6 * math.exp(-0.3 * layer_idx)
    lam = (jnp.exp(jnp.sum(lq1.astype(jnp.float32) * lk1.astype(jnp.float32)))
           - jnp.exp(jnp.sum(lq2.astype(jnp.float32) * lk2.astype(jnp.float32)))
           + lam_init)

    cid = chunk_ids(L)
    outs = [diff_attend(q1[:, :, :N_META], q2[:, :, :N_META], k1[:, :, :N_META],
                        k2[:, :, :N_META], v[:, :, :N_META], cid[:N_META], cid[:N_META], lam)]
    n_blocks = (L - N_META) // Q_BLOCK
    for b in range(n_blocks):
        q0 = N_META + b * Q_BLOCK
        q1e = q0 + Q_BLOCK
        outs.append(diff_attend(q1[:, :, q0:q1e], q2[:, :, q0:q1e], k1[:, :, :q1e],
                                k2[:, :, :q1e], v[:, :, :q1e], cid[q0:q1e], cid[:q1e], lam))
    o = jnp.concatenate(outs, axis=2)
    o = rms_norm(o, subln_g) * (1.0 - lam_init)
    o = o.transpose(0, 2, 1, 3).reshape(B, L, D)
    return (o * jax.nn.silu(g)) @ w_out


def pool_mixer(h, w_in, w_group, scale, w_out):
    B, L, D = h.shape
    proj = h @ w_in
    u, g = jnp.split(proj, 2, axis=-1)
    ug = u.reshape(B, L, POOL_GROUPS, POOL_GROUP_DIM)
    cs = jnp.cumsum(ug.astype(jnp.float32), axis=1)
    csp = jnp.concatenate([jnp.zeros((B, 1, POOL_GROUPS, POOL_GROUP_DIM), jnp.float32), cs], axis=1)
    t = jnp.arange(L)
    pooled = []
    for gi, w in enumerate(POOL_WINDOWS):
        start = jnp.maximum(t + 1 - w, 0)
        cnt = jnp.minimum(t + 1, w).astype(jnp.float32)
        s = csp[:, 1:, gi] - csp[:, start, gi]
        pooled.append(s / cnt[None, :, None])
    pooled = jnp.stack(pooled, axis=2).astype(u.dtype)
    mixed = pooled - ug
    mixed = jnp.einsum('blgc,gcd->blgd', mixed, w_group).reshape(B, L, D) * scale
    return (mixed * jax.nn.silu(g)) @ w_out


def setup_inputs(seed: int = 0) -> dict:
    key = jax.random.key(seed)
    ks = jax.random.split(key, 16)
    D = D_MODEL
    f = jnp.float32
    nrm = jax.random.normal
    return {
        "x": nrm(ks[0], (BATCH, SEQ, D), f),
        "meta_tokens": nrm(ks[1], (N_META, D), f),
        "pre_norm_g": 1.0 + 0.05 * nrm(ks[2], (DEPTH, D), f),
        "post_norm_g": 1.0 + 0.05 * nrm(ks[3], (DEPTH, D), f),
        "attn_w_in": nrm(ks[4], (N_ATTN_LAYERS, D, 4 * D), f) * D ** -0.5,
        "attn_w_out": nrm(ks[5], (N_ATTN_LAYERS, D, D), f) * D ** -0.5,
        "attn_lambda_q1": 0.1 * nrm(ks[6], (N_ATTN_LAYERS, DA_HEAD_DIM), f),
        "attn_lambda_k1": 0.1 * nrm(ks[7], (N_ATTN_LAYERS, DA_HEAD_DIM), f),
        "attn_lambda_q2": 0.1 * nrm(ks[8], (N_ATTN_LAYERS, DA_HEAD_DIM), f),
        "attn_lambda_k2": 0.1 * nrm(ks[9], (N_ATTN_LAYERS, DA_HEAD_DIM), f),
        "attn_subln_g": 1.0 + 0.05 * nrm(ks[10], (N_ATTN_LAYERS, DA_V_DIM), f),
        "pool_w_in": nrm(ks[11], (N_POOL_LAYERS, D, 2 * D), f) * D ** -0.5,
        "pool_w_group": nrm(ks[12], (N_POOL_LAYERS, POOL_GROUPS, POOL_GROUP_DIM, POOL_GROUP_DIM), f) * POOL_GROUP_DIM ** -0.5,
        "pool_scale": 1.0 + 0.1 * nrm(ks[13], (N_POOL_LAYERS, D), f),
        "pool_w_out": nrm(ks[14], (N_POOL_LAYERS, D, D), f) * D ** -0.5,
    }


def reference(x, meta_tokens, pre_norm_g, post_norm_g, attn_w_in, attn_w_out,
              attn_lambda_q1, attn_lambda_k1, attn_lambda_q2, attn_lambda_k2, attn_subln_g,
              pool_w_in, pool_w_group, pool_scale, pool_w_out):
    B = x.shape[0]
    meta = jnp.broadcast_to(meta_tokens.astype(x.dtype)[None], (B, N_META, x.shape[-1]))
    h = jnp.concatenate([meta, x], axis=1)
    L = h.shape[1]
    cos, sin = rope_tables(L)
    for i in range(DEPTH):
        j = i // N_MIXERS
        hn = rms_norm(h, pre_norm_g[i])
        if i % N_MIXERS == 0:
            y = diff_attention_mixer(hn, attn_w_in[j], attn_w_out[j], attn_lambda_q1[j],
                                     attn_lambda_k1[j], attn_lambda_q2[j], attn_lambda_k2[j],
                                     attn_subln_g[j], i, cos, sin)
        else:
            y = pool_mixer(hn, pool_w_in[j], pool_w_group[j], pool_scale[j], pool_w_out[j])
        h = h + rms_norm(y, post_norm_g[i])
    return h[:, N_META:]
```

```python
import math
import numpy as np
import concourse.bass as bass
import concourse.mybir as mybir
from concourse.bass_utils import run_bass_kernel_spmd

F32 = mybir.dt.float32
BF16 = mybir.dt.bfloat16
AF = mybir.ActivationFunctionType
ALU = mybir.AluOpType
AX = mybir.AxisListType

D = 2048
SEQ = 4096
NMETA = 16
L = SEQ + NMETA
EPS = 1e-6
NKC = 16
TC = 1040
LAM_INIT0 = 0.8 - 0.6 * math.exp(-0.3 * 0)
SCALE = 128 ** -0.5
FUSED = True
NT_A = 9


class Buf:
    __slots__ = ("name", "w", "r")

    def __init__(self, name):
        self.name = name
        self.w = None
        self.r = []


class Prog:
    ENG = ("pe", "act", "dve", "pool", "sp")

    def __init__(self, nc):
        self.nc = nc
        self.ops = {e: [] for e in self.ENG}
        self.sem = {e: nc.alloc_semaphore(name=f"s_{e}") for e in self.ENG}
        self.cnt = {e: 0 for e in self.ENG}
        self.waited = {e: {} for e in self.ENG}
        self.dma_sems = {}
        self.pend_r = {e: [] for e in self.ENG}
        self.pend_w = {e: [] for e in self.ENG}

    def _waits(self, eng, deps):
        waits = []
        for d in deps:
            if d is None:
                continue
            if d == "pending":
                assert eng == "pe", "dependency on unresolved pending write"
                continue
            s, v = d
            if eng == "pe" and s is self.sem["pe"]:
                continue
            if self.waited[eng].get(s.num, 0) >= v:
                continue
            self.waited[eng][s.num] = v
            waits.append((s, v))
        return waits

    def _deps(self, reads, writes, extra):
        deps = list(extra)
        for b in reads:
            deps.append(b.w)
        for b in writes:
            deps.extend(b.r)
            deps.append(b.w)
        return deps

    def _commit(self, eng, tok, reads, writes):
        for b in self.pend_r[eng]:
            b.r.append(tok)
        for b in self.pend_w[eng]:
            b.w = tok
        self.pend_r[eng] = []
        self.pend_w[eng] = []
        for b in reads:
            b.r.append(tok)
        for b in writes:
            b.w = tok
            b.r = []

    def do(self, eng, fn, reads=(), writes=(), sig=True, extra=()):
        deps = self._deps(reads, writes, extra)
        waits = self._waits(eng, deps)
        tok = None
        if sig:
            self.cnt[eng] += 1
            tok = (self.sem[eng], self.cnt[eng])
        s_own = self.sem[eng]

        def run(e, fn=fn, waits=waits, sig=sig):
            for (s, v) in waits:
                e.wait_ge(s, v)
            ins = fn(e)
            if sig:
                ins.then_inc(s_own, 1)
        self.ops[eng].append(run)
        if sig:
            self._commit(eng, tok, reads, writes)
        else:
            self.pend_r[eng].extend(reads)
            for b in writes:
                b.r = []
                b.w = "pending"
                self.pend_w[eng].append(b)
        return tok

    def dma(self, eng, out, in_, sem_name, reads=(), writes=(), extra=()):
        if sem_name not in self.dma_sems:
            self.dma_sems[sem_name] = [self.nc.alloc_semaphore(name=f"d_{sem_name}"), 0]
        ent = self.dma_sems[sem_name]
        ent[1] += 16
        s, v = ent[0], ent[1]
        deps = self._deps(reads, writes, extra)
        waits = self._waits(eng, deps)

        def run(e, waits=waits):
            for (ws, wv) in waits:
                e.wait_ge(ws, wv)
            src = in_() if callable(in_) else in_
            e.dma_start(out=out, in_=src).then_inc(s, 16)
        self.ops[eng].append(run)
        tok = (s, v)
        for b in reads:
            b.r.append(tok)
        for b in writes:
            b.w = tok
            b.r = []
        return tok

    def wait(self, eng, deps):
        waits = self._waits(eng, deps)

        def run(e, waits=waits):
            for (ws, wv) in waits:
                e.wait_ge(ws, wv)
        self.ops[eng].append(run)

    def raw(self, eng, fn):
        self.ops[eng].append(fn)

    def emit(self):
        nc = self.nc
        with nc.Block() as block:
            @block.tensor
            def _(e):
                for f in self.ops["pe"]:
                    f(e)

            @block.scalar
            def _(e):
                for f in self.ops["act"]:
                    f(e)

            @block.vector
            def _(e):
                for f in self.ops["dve"]:
                    f(e)

            @block.gpsimd
            def _(e):
                for f in self.ops["pool"]:
                    f(e)

            @block.sync
            def _(e):
                for f in self.ops["sp"]:
                    f(e)


class Arena:
    def __init__(self, nc, nbytes, name):
        self.t = nc.alloc_sbuf_tensor(name, [128, nbytes // 4], F32)
        self.cap = nbytes
        self.off = 0

    def reset(self, off=0):
        self.off = off

    def alloc(self, shape, dtype):
        n = 1
        for s in shape:
            n *= s
        sz = 2 if dtype == BF16 else 4
        nb = (n * sz + 31) // 32 * 32
        assert self.off + nb <= self.cap, f"arena overflow {self.off}+{nb}>{self.cap}"
        v = self.t[:, self.off // 4:(self.off + nb) // 4]
        if dtype == BF16:
            v = v.bitcast(BF16)
        v = v[:, 0:n]
        self.off += nb
        if len(shape) == 2:
            return v.rearrange("p (a b) -> p a b", a=shape[0])
        if len(shape) == 3:
            return v.rearrange("p (a b c) -> p a b c", a=shape[0], b=shape[1])
        return v


def build_A(nc, P, ar, pb, io, og_slots, after_store=None, after_last_proj=None):
    hTv = io["hT"].rearrange("(c p) t -> p c t", p=128)
    wAv = io["wA"].rearrange("(c p) n -> p c n", p=128)
    if isinstance(og_slots, list):
        ogv_t = [h.ap().rearrange("(c p) t -> p c t", p=128) for h in og_slots]
    else:
        ogv_t = [og_slots[t_].rearrange("(c p) t -> p c t", p=128) for t_ in range(9)]

    wA = ar.alloc([NKC, 2048], BF16)
    xraw = ar.alloc([4096], F32)
    xb = xraw.bitcast(BF16).rearrange("p (c t) -> p c t", c=NKC)
    sq = ar.alloc([NKC, 512], BF16)
    dead_mark = ar.off
    kT = ar.alloc([4, L], BF16)
    Vp = ar.alloc([33, 2, 257], BF16)
    qT = ar.alloc([4, 512], BF16)
    sg = ar.alloc([4, 512], F32)
    rstd_fm = ar.alloc([512], F32)
    rstd_tm = ar.alloc([4], F32)
    cosb = ar.alloc([512], F32)
    sinb = ar.alloc([512], F32)
    t1 = ar.alloc([512], F32)
    t2 = ar.alloc([512], F32)
    NE = 5
    e_sb = [ar.alloc([512], BF16) for _ in range(NE)]
    o_sb = ar.alloc([4, 256], F32)
    og_sb = ar.alloc([4, 512], BF16)
    ogT_st = ar.alloc([4, 512], BF16)
    ident = ar.alloc([128], BF16)
    perm = ar.alloc([32], BF16)
    onesM = ar.alloc([128], BF16)
    onesF = ar.alloc([128], F32)
    gpre = ar.alloc([NKC], F32)
    lamv = ar.alloc([4], F32)
    lamt = ar.alloc([8], F32)
    sublnG = ar.alloc([256], F32)
    small = ar.alloc([64], F32)
    epsc = ar.alloc([8], F32)
    tmp256 = ar.alloc([256], F32)
    tmp256b = ar.alloc([256], F32)
    wstg = [xraw[:, 0:2048], xraw[:, 2048:4096]]

    B = {}

    def buf(name):
        if name not in B:
            B[name] = Buf(name)
        return B[name]

    bank = [buf(f"bank{i}") for i in range(8)]

    P.dma("sp", gpre, io["gpre0"], "c0", writes=[buf("gpre")])
    P.dma("sp", lamv, io["lamv"], "c1", writes=[buf("lamv")])
    P.dma("sp", sublnG, io["sublnG"], "c2", writes=[buf("sublnG")])
    P.dma("pool", ident, io["ident"], "c3", writes=[buf("ident")])
    P.dma("pool", perm[0:32, :], io["perm"], "c4", writes=[buf("perm")])
    P.do("dve", lambda e: e.memset(onesM, 2.0 ** -11), writes=[buf("onesM")])
    P.do("dve", lambda e: e.memset(onesF, 1.0), writes=[buf("onesF")])
    P.do("pool", lambda e: e.memset(Vp[:, :, :, 256:257], 1.0), writes=[buf("Vp")])
    P.do("dve", lambda e: e.memset(epsc[:, 0:1], EPS), writes=[buf("epsc")])
    P.do("dve", lambda e: e.memset(epsc[:, 1:2], math.log(1.0 - LAM_INIT0)), writes=[buf("epsc")])

    P.do("dve", lambda e: e.tensor_tensor(out=lamt[:, 0:1], in0=lamv[:, 0:1], in1=lamv[:, 1:2], op=ALU.mult),
         reads=[buf("lamv")], writes=[buf("lamt")])
    P.do("dve", lambda e: e.tensor_tensor(out=lamt[:, 1:2], in0=lamv[:, 2:3], in1=lamv[:, 3:4], op=ALU.mult),
         reads=[buf("lamv")], writes=[buf("lamt")])
    P.do("pe", lambda e: e.matmul(pb[7][:, 0:2], lhsT=onesF, rhs=lamt[:, 0:2], start=True, stop=True),
         reads=[buf("onesF"), buf("lamt")], writes=[bank[7]])
    P.do("act", lambda e: e.activation(out=lamt[:, 2:4], in_=pb[7][:, 0:2], func=AF.Exp),
         reads=[bank[7]], writes=[buf("lamt2")])
    P.do("dve", lambda e: e.scalar_tensor_tensor(out=lamt[:, 4:5], in0=lamt[:, 3:4], scalar=-LAM_INIT0,
                                                 in1=lamt[:, 2:3], op0=ALU.add, op1=ALU.subtract),
         reads=[buf("lamt2")], writes=[buf("neglam")])
    neglam = lamt[:, 4:5]

    WT = []
    for kc in range(NKC):
        sb = buf(f"wstg{kc % 2}")
        P.dma("sp", wstg[kc % 2], wAv[:, kc, :], f"wstg{kc % 2}", writes=[sb])
        wt = P.do("dve", lambda e, kc=kc: e.tensor_scalar(out=wA[:, kc, :], in0=wstg[kc % 2],
                                                          scalar1=gpre[:, kc:kc + 1], scalar2=None, op0=ALU.mult),
                  reads=[sb, buf("gpre")], writes=[buf("wA")])
        WT.append(wt)

    tiles = [(0, 16)] + [(16 + 512 * i, 512) for i in range(8)]

    def tile_blocks(t):
        if t == 0:
            return [(0, 0, 16)]
        return [(4 * (t - 1) + 1 + j, 128 * j, 128) for j in range(4)]

    wtoks = WT[-2:]

    def prep_load(t):
        pos0, NT = tiles[t]
        for q4 in range(4):
            P.dma("pool", xb[:, 4 * q4:4 * q4 + 4, :NT], hTv[:, 4 * q4:4 * q4 + 4, pos0:pos0 + NT], f"xb{q4}",
                  writes=[buf(f"xb{q4}")], extra=(wtoks if t == 0 else ()))
        P.dma("sp", cosb[0:32, :NT], io["cosT"][:, pos0:pos0 + NT], "cos", writes=[buf("cos")])
        P.dma("sp", sinb[0:32, :NT], io["sinT"][:, pos0:pos0 + NT], "sin", writes=[buf("sin")])

    def prep_squares(t):
        pos0, NT = tiles[t]
        for q4 in range(4):
            P.do("pool", lambda e, q4=q4: e.tensor_tensor(out=sq[:, 4 * q4:4 * q4 + 4, :NT],
                                                          in0=xb[:, 4 * q4:4 * q4 + 4, :NT],
                                                          in1=xb[:, 4 * q4:4 * q4 + 4, :NT], op=ALU.mult),
                 reads=[buf(f"xb{q4}")], writes=[buf(f"sq{q4}")])

    def prep_stats(t):
        pos0, NT = tiles[t]
        blks = tile_blocks(t)
        for q4 in range(4):
            sqb = buf(f"sq{q4}")
            for k4 in range(4):
                kc = 4 * q4 + k4
                P.do("pe", lambda e, kc=kc: e.matmul(pb[6][:, :NT], lhsT=onesM, rhs=sq[:, kc, :NT],
                                                     start=(kc == 0), stop=(kc == NKC - 1)),
                     reads=[sqb, buf("onesM")], writes=[bank[6]], sig=False)
                for bi, (kb, off, ntok) in enumerate(blks):
                    last = (bi == len(blks) - 1 and k4 == 3)
                    P.do("pe", lambda e, kc=kc, bi=bi, off=off, ntok=ntok: e.matmul(
                        pb[7][:ntok, bi:bi + 1], lhsT=sq[:, kc, off:off + ntok], rhs=onesM[:, 0:1],
                        start=(kc == 0 and bi == 0), stop=(kc == NKC - 1), skip_group_check=True),
                        reads=[sqb, buf("onesM")], writes=[bank[7]], sig=last)
        P.do("act", lambda e: e.activation(out=rstd_fm[:, :NT], in_=pb[6][:, :NT], func=AF.Ln, bias=epsc[:, 0:1]),
             reads=[bank[6], buf("epsc")], writes=[buf("rstd_fm")])
        P.do("act", lambda e: e.activation(out=rstd_fm[:, :NT], in_=rstd_fm[:, :NT], func=AF.Exp, scale=-0.5),
             reads=[buf("rstd_fm")], writes=[buf("rstd_fm")])
        nb = len(blks)
        np_ = blks[0][2]
        P.do("act", lambda e: e.activation(out=rstd_tm[:np_, :nb], in_=pb[7][:np_, :nb], func=AF.Ln,
                                           bias=epsc[:np_, 0:1]),
             reads=[bank[7], buf("epsc")], writes=[buf("rstd_tm")])
        P.do("act", lambda e: e.activation(out=rstd_tm[:np_, :nb], in_=rstd_tm[:np_, :nb], func=AF.Exp, scale=-0.5),
             reads=[buf("rstd_tm")], writes=[buf("rstd_tm")])

    pj = [0]

    def proj(t):
        pos0, NT = tiles[t]
        pj[0] = 0
        xbufs = [buf(f"xb{q}") for q in range(4)]
        ropes = []

        def do_rope(dst, dbuf):
            d32 = dst[0:32, :]
            P.do("pe", lambda e, d32=d32: e.matmul(pb[7][0:32, :NT], lhsT=perm[0:32, 0:32], rhs=d32,
                                                   start=True, stop=True),
                 reads=[dbuf, buf("perm")], writes=[bank[7]])
            P.do("pool", lambda e, d32=d32: e.tensor_tensor(out=t1[0:32, :NT], in0=d32, in1=cosb[0:32, :NT],
                                                            op=ALU.mult),
                 reads=[dbuf, buf("cos")], writes=[buf("t1")])
            P.do("dve", lambda e: e.tensor_tensor(out=t2[0:32, :NT], in0=pb[7][0:32, :NT], in1=sinb[0:32, :NT],
                                                  op=ALU.mult),
                 reads=[bank[7], buf("sin")], writes=[buf("t2")])
            P.do("pool", lambda e, d32=d32: e.tensor_tensor(out=d32, in0=t1[0:32, :NT], in1=t2[0:32, :NT],
                                                            op=ALU.add),
                 reads=[buf("t1"), buf("t2")], writes=[dbuf])

        for oc in range(8):
            bk = pj[0] % 4
            pj[0] += 1
            for kc in range(NKC):
                P.do("pe", lambda e, kc=kc, oc=oc, bk=bk: e.matmul(
                    pb[bk][:, :NT], lhsT=wA[:, kc, oc * 128:(oc + 1) * 128], rhs=xb[:, kc, :NT],
                    start=(kc == 0), stop=(kc == NKC - 1)),
                    reads=[buf("wA"), xbufs[kc // 4]], writes=[bank[bk]], sig=(kc == NKC - 1))
            if oc < 4:
                dst = qT[:, oc, :NT]
                dbuf = buf(f"qT{oc}")
            else:
                dst = kT[:, oc - 4, pos0:pos0 + NT]
                dbuf = buf(f"kT{oc - 4}")
            P.do("dve", lambda e, dst=dst, bk=bk: e.tensor_tensor(out=dst, in0=pb[bk][:, :NT], in1=rstd_fm[:, :NT],
                                                                  op=ALU.mult),
                 reads=[bank[bk], buf("rstd_fm")], writes=[dbuf])
            ropes.append((dst, dbuf))
            if len(ropes) > 1:
                do_rope(*ropes.pop(0))
            if oc == 0:
                advance_subln(1)
            if oc == 2:
                advance_subln(2)
            if oc == 6:
                flush_deferred()
        blks = tile_blocks(t)
        for bi, (kb, off, ntok) in enumerate(blks):
            for which in range(2):
                bk = pj[0] % 4
                pj[0] += 1
                c0 = 1024 + 512 * which
                for kc in range(NKC):
                    P.do("pe", lambda e, kc=kc, bk=bk, off=off, ntok=ntok, c0=c0: e.matmul(
                        pb[bk][:ntok, 0:512], lhsT=xb[:, kc, off:off + ntok], rhs=wA[:, kc, c0:c0 + 512],
                        start=(kc == 0), stop=(kc == NKC - 1)),
                        reads=[buf("wA"), xbufs[kc // 4]], writes=[bank[bk]], sig=(kc == NKC - 1))
                if which == 0:
                    P.do("act", lambda e, bk=bk, kb=kb, bi=bi, ntok=ntok: e.activation(
                        out=Vp[:ntok, kb, :, 0:256], in_=pb[bk][:ntok, 0:512].rearrange("p (h d) -> p h d", h=2),
                        func=AF.Copy, scale=rstd_tm[:ntok, bi:bi + 1]),
                        reads=[bank[bk], buf("rstd_tm")], writes=[buf("Vp")])
                else:
                    P.do("act", lambda e, bk=bk, bi=bi, ntok=ntok: e.activation(
                        out=sg[:ntok, bi, :], in_=pb[bk][:ntok, 0:512], func=AF.Silu,
                        scale=rstd_tm[:ntok, bi:bi + 1]),
                        reads=[bank[bk], buf("rstd_tm")], writes=[buf("sg")])
            if bi == 0:
                while ropes:
                    do_rope(*ropes.pop(0))

    deferred = []
    pending_subln = []

    def advance_subln(upto):
        for d in pending_subln:
            if d["done"] < 1 and upto >= 1:
                d["A"]()
                d["done"] = 1
            if d["done"] < 2 and upto >= 2:
                d["B"]()
                d["done"] = 2
        pending_subln[:] = [d for d in pending_subln if d["done"] < 2]

    def flush_deferred():
        advance_subln(2)
        while deferred:
            deferred.pop(0)()

    sc = [0]
    ec = [0]

    def attn_head(t, h):
        pos0, NT = tiles[t]
        blks = tile_blocks(t)
        nqb = len(blks)
        nq = blks[0][2]
        keys = [(0, 0, 16, 0)]
        if t > 0:
            for kb in range(1, 4 * (t - 1) + 1):
                keys.append((kb, 16 + 128 * (kb - 1), 128, 0))
            for j in range(4):
                kb = 4 * (t - 1) + 1 + j
                keys.append((kb, 16 + 128 * (kb - 1), 128, j))
        LOOK = 3
        items = [(m, ki) + keys[ki] for m in range(2) for ki in range(len(keys))]

        def emit_S(it):
            m, ki, kb, kpos, nk, jv = it
            qi = 2 * h + m
            c0 = jv * 128
            ncols = NT - c0
            sbk = 4 + (sc[0] % 4)
            sc[0] += 1
            ei = ec[0] % NE
            ec[0] += 1
            eb = buf(f"e{ei}")
            P.do("pe", lambda e, sbk=sbk, qi=qi, kpos=kpos, nk=nk, c0=c0, ncols=ncols: e.matmul(
                pb[sbk][:nk, :ncols], lhsT=kT[:, qi, kpos:kpos + nk], rhs=qT[:, qi, c0:c0 + ncols],
                start=True, stop=True),
                reads=[buf(f"kT{qi}"), buf(f"qT{qi}")], writes=[bank[sbk]])
            P.do("act", lambda e, sbk=sbk, ei=ei, nk=nk, ncols=ncols: e.activation(
                out=e_sb[ei][:nk, :ncols], in_=pb[sbk][:nk, :ncols], func=AF.Exp, scale=SCALE),
                reads=[bank[sbk]], writes=[eb])
            if t > 0 and kb >= 4 * (t - 1) + 1:
                P.do("dve", lambda e, ei=ei: e.memset(e_sb[ei][64:128, 0:64], 0.0), writes=[eb])
            return ei

        def emit_PV(it, ei):
            m, ki, kb, kpos, nk, jv = it
            eb = buf(f"e{ei}")
            for j in range(jv, nqb):
                lc = (j - jv) * 128 if t > 0 else 0
                first = (ki == 0)
                last = (kb == blks[j][0])
                lastpv = (j == nqb - 1)
                P.do("pe", lambda e, ei=ei, nk=nk, lc=lc, j=j, kb=kb, first=first, last=last: e.matmul(
                    pb[j][:nq, 0:257], lhsT=e_sb[ei][:nk, lc:lc + nq], rhs=Vp[:nk, kb, h, :],
                    start=first, stop=last),
                    reads=[eb, buf("Vp")], writes=[bank[j]], sig=(last or lastpv))
                if last:
                    evac(m, j)

        def evac(m, j):
            if j == 0:
                advance_subln(2)
            if True:
                sm = small[:nq, 4 * j:4 * j + 4]
                sbf = buf(f"small{j}")
                P.do("dve", lambda e, j=j, sm=sm: e.reciprocal(out=sm[:, 0:1], in_=pb[j][:nq, 256:257]),
                     reads=[bank[j]], writes=[sbf])
                if m == 0:
                    P.do("dve", lambda e, j=j, sm=sm: e.tensor_scalar(out=o_sb[:nq, j, :], in0=pb[j][:nq, 0:256],
                                                                      scalar1=sm[:, 0:1], scalar2=None, op0=ALU.mult),
                         reads=[bank[j], sbf], writes=[buf(f"o{j}")])
                else:
                    P.do("dve", lambda e, sm=sm: e.tensor_tensor(out=sm[:, 1:2], in0=sm[:, 0:1], in1=neglam[:nq, :],
                                                                 op=ALU.mult),
                         reads=[sbf, buf("neglam")], writes=[sbf])
                    P.do("dve", lambda e, j=j, sm=sm: e.scalar_tensor_tensor(
                        out=o_sb[:nq, j, :], in0=pb[j][:nq, 0:256], scalar=sm[:, 1:2], in1=o_sb[:nq, j, :],
                        op0=ALU.mult, op1=ALU.add),
                        reads=[bank[j], sbf, buf(f"o{j}")], writes=[buf(f"o{j}")])

        pend = []
        for n_it, it in enumerate(items):
            ei = emit_S(it)
            pend.append((it, ei))
            if len(pend) > LOOK:
                emit_PV(*pend.pop(0))
            if n_it == 3:
                advance_subln(1)
            if n_it == 9:
                advance_subln(2)
        while pend:
            emit_PV(*pend.pop(0))
        pending_subln.append({"A": lambda: subln_A(t, h), "B": lambda: subln_B(t, h), "done": 0})

    def subln_A(t, h):
        blks = tile_blocks(t)
        nq = blks[0][2]
        for j in range(len(blks)):
            sm = small[:nq, 16 + 4 * j:16 + 4 * j + 4]
            P.do("pool", lambda e, j=j: e.tensor_tensor(out=tmp256[:nq, :], in0=o_sb[:nq, j, :], in1=o_sb[:nq, j, :],
                                                        op=ALU.mult),
                 reads=[buf(f"o{j}")], writes=[buf("tmp256")])
            P.do("dve", lambda e, sm=sm: e.tensor_reduce(out=sm[:, 0:1], in_=tmp256[:nq, :], axis=AX.X, op=ALU.add),
                 reads=[buf("tmp256")], writes=[buf(f"small2{j}")])

    def subln_B(t, h):
        blks = tile_blocks(t)
        nq = blks[0][2]
        for j in range(len(blks)):
            sm = small[:nq, 16 + 4 * j:16 + 4 * j + 4]
            sbf = buf(f"small2{j}")
            P.do("act", lambda e, sm=sm: e.activation(out=sm[:, 1:2], in_=sm[:, 0:1], func=AF.Ln, scale=1.0 / 256,
                                                      bias=epsc[:nq, 0:1]),
                 reads=[sbf, buf("epsc")], writes=[sbf])
            P.do("act", lambda e, sm=sm: e.activation(out=sm[:, 2:3], in_=sm[:, 1:2], func=AF.Exp, scale=-0.5,
                                                      bias=epsc[:nq, 1:2]),
                 reads=[sbf, buf("epsc")], writes=[sbf])
        for j in range(len(blks)):
            sm = small[:nq, 16 + 4 * j:16 + 4 * j + 4]
            sbf = buf(f"small2{j}")
            P.do("dve", lambda e, j=j, sm=sm: e.scalar_tensor_tensor(
                out=tmp256b[:nq, :], in0=o_sb[:nq, j, :], scalar=sm[:, 2:3], in1=sublnG[:nq, :],
                op0=ALU.mult, op1=ALU.mult),
                reads=[buf(f"o{j}"), sbf, buf("sublnG")], writes=[buf("tmp256b")])
            P.do("pool", lambda e, j=j: e.tensor_tensor(out=og_sb[:nq, j, h * 256:(h + 1) * 256], in0=tmp256b[:nq, :],
                                                        in1=sg[:nq, j, h * 256:(h + 1) * 256], op=ALU.mult),
                 reads=[buf("tmp256b"), buf("sg")], writes=[buf(f"og{j}")])

    def finish_tile(t):
        pos0, NT = tiles[t]
        blks = tile_blocks(t)
        nq = blks[0][2]
        for j in range(len(blks)):
            tb = 4 + (j % 2)
            pbT = pb[tb][:, :].bitcast(BF16)
            for c in range(4):
                P.do("pe", lambda e, j=j, c=c, pbT=pbT: e.transpose(pbT[:, c * 128:c * 128 + nq],
                                                                    og_sb[:nq, j, c * 128:(c + 1) * 128],
                                                                    ident[:nq, :nq]),
                     reads=[buf(f"og{j}"), buf("ident")], writes=[bank[tb]], sig=(c == 3))
            P.do("dve", lambda e, j=j, pbT=pbT: e.tensor_copy(
                out=ogT_st[:, :, j * 128:j * 128 + nq],
                in_=pbT[:, 0:512].rearrange("p (c q) -> p c q", c=4)[:, :, 0:nq]),
                reads=[bank[tb]], writes=[buf("ogT_st")])
        dst = ogv_t[t][:, :, 512 - NT:512]
        tok = P.dma("sp", dst, ogT_st[:, :, :NT], "ogst", reads=[buf("ogT_st")])
        if after_store is not None:
            after_store(t, tok)
        return tok

    P.do("dve", lambda e: e.memset(ogT_st, 0.0), writes=[buf("ogT_st")])
    P.dma("sp", ogv_t[0][:, :, 0:496], ogT_st[:, :, 0:496], "ogz", reads=[buf("ogT_st")])
    out_toks = []
    prep_load(0)
    prep_squares(0)
    prep_stats(0)
    for t in range(NT_A):
        proj(t)
        if t + 1 < NT_A:
            prep_load(t + 1)
            prep_squares(t + 1)
        elif after_last_proj is not None:
            assert not P.pend_r["pe"]
            deadb = [buf("wA")] + [buf(f"xb{q}") for q in range(4)] + [buf(f"sq{q}") for q in range(4)]
            toks_ = []
            for b_ in deadb:
                toks_.extend(b_.r)
                toks_.append(b_.w)
            after_last_proj(toks_, dead_mark)
        attn_head(t, 0)
        if t + 1 < NT_A:
            prep_stats(t + 1)
        attn_head(t, 1)
        deferred.append(lambda t=t: out_toks.append(finish_tile(t)))
    flush_deferred()
    return out_toks


class WStream:
    NW = 4
    PREF = 3

    def __init__(self, P, ar, io):
        self.P = P
        self.wbuf = [ar.alloc([NKC, 256], BF16) for _ in range(self.NW)]
        wo0 = io["wo0"].rearrange("(c p) n -> p c n", p=128)
        wi1 = io["wi1"].rearrange("(c p) n -> p c n", p=128)
        wgv = io["wg"].rearrange("g (c p) n -> p g c n", p=128)
        wo1 = io["wo1"].rearrange("(c p) n -> p c n", p=128)
        wl = []
        for rep in range(2):
            for cb_ in range(8):
                wl.append(wo0[:, :, cb_ * 256:(cb_ + 1) * 256])
        for gi in range(4):
            for hb in range(2):
                wl.append(wi1[:, :, gi * 512 + hb * 256:gi * 512 + (hb + 1) * 256])
            for hb in range(2):
                wl.append(wi1[:, :, 2048 + gi * 512 + hb * 256:2048 + gi * 512 + (hb + 1) * 256])
                wl.append(wgv[:, gi, :, hb * 256:(hb + 1) * 256])
        for rep in range(2):
            for cb_ in range(8):
                wl.append(wo1[:, :, cb_ * 256:(cb_ + 1) * 256])
        self.wlist = wl
        self.bufs = [Buf(f"w{i}") for i in range(self.NW)]
        self.issued = 0
        self.next = 0

    def issue_upto(self, n, extra=()):
        while self.issued < min(len(self.wlist), n):
            idx = self.issued
            v = self.wlist[idx]
            i = idx % self.NW
            self.P.dma("pool", self.wbuf[i][:, :v.shape[1], :], v, f"w{i}", writes=[self.bufs[i]], extra=extra)
            self.issued += 1

    def get(self):
        idx = self.next
        self.next += 1
        self.issue_upto(idx + self.PREF)
        return idx % self.NW, self.bufs[idx % self.NW]


def build_C(nc, P, ar, pb, io, ogT_src, h1_scr, out_dst, deps_in, ws=None, og_pre=None):
    if callable(ogT_src):
        og_src = ogT_src
    else:
        ogv = ogT_src.rearrange("(c p) t -> p c t", p=128)

        def og_src(q4, part):
            if part == 0:
                return ogv[:, 4 * q4:4 * q4 + 4, 0:16]
            return ogv[:, 4 * q4:4 * q4 + 4, 16 + 512 * (part - 1):16 + 512 * part]
    h0v = io["hTs"].rearrange("(c p) t -> p c t", p=128)
    h1v = h1_scr.rearrange("(c p) t -> p c t", p=128)
    outv = out_dst.rearrange("(c p) t -> p c t", p=128)
    CP = [(0, 512), (512, 512), (1024, 16)]
    CPO = [(16, 512), (528, 512)]
    PO = [(0, 512), (512, 512)]

    B = {}

    def buf(name):
        if name not in B:
            B[name] = Buf(name)
        return B[name]

    bank = [buf(f"cbank{i}") for i in range(8)]
    if ws is None:
        ws = WStream(P, ar, io)
    else:
        ar.reset(ws.NW * NKC * 256 * 2)
    wbuf = ws.wbuf
    get_w = ws.get
    ogT = ar.alloc([NKC, TC], BF16)
    zT = ogT
    ymark = ar.off
    ybuf = ar.alloc([NKC, TC], F32)
    yend = ar.off
    h1n = ar.alloc([NKC, TC], BF16)
    NS = 3
    sqc = [ar.alloc([TC], BF16) for _ in range(4)]
    stg = [ar.alloc([TC], F32) for _ in range(NS)]
    h1c = [ar.alloc([TC], F32) for _ in range(NS)]
    rstd = ar.alloc([TC], F32)
    onesM = ar.alloc([128], BF16)
    gv = ar.alloc([4, NKC], F32)
    epsc = ar.alloc([8], F32)
    endmark = ar.off
    ar.reset(ymark)
    u_sb = [ar.alloc([TC], F32) for _ in range(2)]
    s2 = [ar.alloc([TC], F32) for _ in range(2)]
    s4 = [ar.alloc([TC], F32) for _ in range(2)]
    mixed = ar.alloc([4, 1024], BF16)
    sgc = [ar.alloc([1024], F32) for _ in range(2)]
    assert ar.off <= yend
    ar.reset(endmark)

    P.do("dve", lambda e: e.memset(onesM, 2.0 ** -11), writes=[buf("onesM")])
    P.do("dve", lambda e: e.memset(epsc[:, 0:1], EPS), writes=[buf("epsc")])
    for i, nm in enumerate(["gpost0", "gpre1", "pscale", "gpost1"]):
        P.dma("sp", gv[:, i, :], io[nm], f"cg{i}", writes=[buf("gv")])
    if og_pre is not None:
        for q4 in range(4):
            B[f"ogT{q4}"] = og_pre["bufs"][q4]
    for q4 in range(4):
        for part in (0, 1, 2):
            if og_pre is not None and part in og_pre["parts"]:
                continue
            c_lo, c_hi = (0, 16) if part == 0 else (16 + 512 * (part - 1), 16 + 512 * part)
            P.dma("sp", ogT[:, 4 * q4:4 * q4 + 4, c_lo:c_hi], (lambda q4=q4, part=part: og_src(q4, part)),
                  f"ogin{part}_{q4}", writes=[buf(f"ogTB{q4}" if part == 2 else f"ogT{q4}")], extra=deps_in)

    cb = [0]

    def next_bank():
        b = cb[0] % 5
        cb[0] += 1
        return b

    def mm_chunk(wi, wbf, wcol, act, act_bufs, pieces, nk=NKC):
        res = []
        for (c0, n) in pieces:
            bk = next_bank()
            for kc in range(nk):
                P.do("pe", lambda e, kc=kc, bk=bk, c0=c0, n=n: e.matmul(
                    pb[bk][:, :n], lhsT=wbuf[wi][:, kc, wcol:wcol + 128], rhs=act[:, kc, c0:c0 + n],
                    start=(kc == 0), stop=(kc == nk - 1)),
                    reads=[wbf] + act_bufs, writes=[bank[bk]], sig=(kc == nk - 1))
            res.append((bk, c0, n))
        return res

    def stats_add(stream, k, src, src_buf, pieces, total):
        i = 2 * stream + (k % 2)
        sb = buf(f"sqc{i}")
        lo = pieces[0][0]
        hi = pieces[-1][0] + pieces[-1][1]
        while len(stat_pend) > stat_depth[0]:
            stat_pend.pop(0)()
        P.do("act", lambda e, i=i: e.activation(out=sqc[i][:, lo:hi], in_=src[:, lo:hi], func=AF.Square),
             reads=[src_buf], writes=[sb])

        def pe_part():
            for pi, (c0, n, sbk) in enumerate(pieces):
                P.do("pe", lambda e, i=i, c0=c0, n=n, k=k, sbk=sbk: e.matmul(
                    pb[sbk][:, :n], lhsT=onesM, rhs=sqc[i][:, c0:c0 + n], start=(k == 0), stop=(k == total - 1),
                    skip_group_check=True),
                    reads=[sb, buf("onesM")], writes=[bank[sbk]], sig=(pi == len(pieces) - 1))
        stat_pend.append(pe_part)

    stat_pend = []
    stat_depth = [1]

    def stats_finish(pieces):
        while stat_pend:
            stat_pend.pop(0)()
        for (c0, n, sbk) in pieces:
            P.do("act", lambda e, sbk=sbk, c0=c0, n=n: e.activation(out=rstd[:, c0:c0 + n], in_=pb[sbk][:, :n],
                                                                    func=AF.Ln, bias=epsc[:, 0:1]),
                 reads=[bank[sbk], buf("epsc")], writes=[buf("rstd")])
            P.do("act", lambda e, c0=c0, n=n: e.activation(out=rstd[:, c0:c0 + n], in_=rstd[:, c0:c0 + n],
                                                           func=AF.Exp, scale=-0.5),
                 reads=[buf("rstd")], writes=[buf("rstd")])

    SP_A = [(0, 16, 5), (16, 512, 6)]
    SP_B = [(528, 512, 7)]
    SP_ALL = [(0, 512, 6), (512, 512, 7), (1024, 16, 5)]
    SP_O = [(0, 512, 6), (512, 512, 7)]
    ogbufs_s = [[buf(f"ogT{q}") for q in range(4)], [buf(f"ogTB{q}") for q in range(4)]]
    ybufs = [buf(f"y{m}") for m in range(NKC)]
    h1d = [buf(f"h1d{m}") for m in range(NKC)]

    def c1_chunk(wi, wbf, ml, m, spieces, stream):
        res = mm_chunk(wi, wbf, ml * 128, ogT, ogbufs_s[stream], [(c0, n) for (c0, n, _) in spieces])
        for (bk, c0, n) in res:
            P.do("act", lambda e, bk=bk, c0=c0, n=n, m=m: e.activation(out=ybuf[:, m, c0:c0 + n],
                                                                         in_=pb[bk][:, :n], func=AF.Copy),
                 reads=[bank[bk]], writes=[ybufs[m]])
        stats_add(stream, m, ybuf[:, m, :], ybufs[m], spieces, NKC)

    ldc = [0]
    spill_toks = []
    last_spill = {}
    a_toks = []

    def ld_h0(m, lo, hi):
        i = ldc[0] % NS
        ldc[0] += 1
        P.dma("sp", stg[i][:, lo:hi], h0v[:, m, lo:hi], f"stg{i}", writes=[buf(f"stg{i}")])
        return i

    def c2_chunk(i, m, lo, hi, spieces, stream, k_add):
        sb = buf(f"stg{i}")
        hb = buf(f"h1c{i}")
        P.do("dve", lambda e: e.scalar_tensor_tensor(out=h1c[i][:, lo:hi], in0=ybuf[:, m, lo:hi],
                                                     scalar=gv[:, 0, m:m + 1], in1=rstd[:, lo:hi],
                                                     op0=ALU.mult, op1=ALU.mult),
             reads=[ybufs[m], buf("gv"), buf("rstd")], writes=[hb])
        a_toks.append(P.do("pool" if k_add % 3 != 2 else "dve", lambda e: e.tensor_tensor(
            out=ybuf[:, m, lo:hi], in0=h1c[i][:, lo:hi], in1=stg[i][:, lo:hi], op=ALU.add),
            reads=[sb, hb], writes=[ybufs[m]]))
        spill_toks.append(P.dma("sp", h1v[:, m, lo:hi], ybuf[:, m, lo:hi], f"h1st{m % 4}", reads=[ybufs[m]],
                                writes=[h1d[m]], extra=last_spill.get(m % 4, [])))
        last_spill[m % 4] = [spill_toks[-1]]
        stats_add(stream, m, ybuf[:, m, :], ybufs[m], spieces, NKC)

    for cb_ in range(8):
        wi, wbf = get_w()
        for ml in range(2):
            c1_chunk(wi, wbf, ml, cb_ * 2 + ml, SP_A, 0)
    stats_finish(SP_A)
    def pass2(m, lo, hi):
        P.do("dve", lambda e: e.scalar_tensor_tensor(
            out=h1n[:, m, lo:hi], in0=ybuf[:, m, lo:hi], scalar=gv[:, 1, m:m + 1], in1=rstd[:, lo:hi],
            op0=ALU.mult, op1=ALU.mult),
            reads=[ybufs[m], buf("gv"), buf("rstd")], writes=[buf(f"h1n{m // 4}")])

    pre = [ld_h0(m, 0, 528) for m in range(NS - 1)]
    ca = 0
    cp2 = 0
    stat_depth[0] = 2
    for cb_ in range(8):
        wi, wbf = get_w()
        for ml in range(2):
            m = cb_ * 2 + ml
            c1_chunk(wi, wbf, ml, m, SP_B, 1)
            for _ in range(2):
                if ca < NKC:
                    if ca + NS - 1 < NKC:
                        pre.append(ld_h0(ca + NS - 1, 0, 528))
                    c2_chunk(pre.pop(0), ca, 0, 528, SP_A, 0, ca)
                    ca += 1
            if m == 7:
                stats_finish(SP_A)
                stat_depth[0] = 1
            if m >= 8:
                for _ in range(2):
                    pass2(cp2, 0, 528)
                    cp2 += 1
    assert ca == NKC and cp2 == NKC
    stats_finish(SP_B)
    NB6 = 2 * NS
    stgB = [stg[k // 2][:, (k % 2) * 520:(k % 2) * 520 + 512] for k in range(NB6)]
    h1cB = [h1c[k // 2][:, (k % 2) * 520:(k % 2) * 520 + 512] for k in range(NB6)]
    a_half = list(a_toks)
    b_toks = []

    def ld_h0B(m):
        k = m % NB6
        P.dma("sp", stgB[k], h0v[:, m, 528:TC], f"stgB{k}", writes=[buf(f"stgB{k}")], extra=a_half)

    def c2_chunk_B(m):
        k = m % NB6
        sb = buf(f"stgB{k}")
        hb = buf(f"h1cB{k}")
        P.do("dve", lambda e: e.scalar_tensor_tensor(out=h1cB[k], in0=ybuf[:, m, 528:TC], scalar=gv[:, 0, m:m + 1],
                                                     in1=rstd[:, 528:TC], op0=ALU.mult, op1=ALU.mult),
             reads=[ybufs[m], buf("gv"), buf("rstd")], writes=[hb], extra=a_half)
        b_toks.append(P.do("pool" if m % 3 != 2 else "dve", lambda e: e.tensor_tensor(
            out=ybuf[:, m, 528:TC], in0=h1cB[k], in1=stgB[k], op=ALU.add),
            reads=[sb, hb], writes=[ybufs[m]]))
        spill_toks.append(P.dma("sp", h1v[:, m, 528:TC], ybuf[:, m, 528:TC], f"h1st{m % 4}", reads=[ybufs[m]],
                                writes=[h1d[m]], extra=last_spill.get(m % 4, [])))
        last_spill[m % 4] = [spill_toks[-1]]
        stats_add(1, m, ybuf[:, m, :], ybufs[m], SP_B, NKC)

    for m in range(NB6 - 1):
        ld_h0B(m)
    for m in range(NKC):
        if m + NB6 - 1 < NKC:
            ld_h0B(m + NB6 - 1)
        c2_chunk_B(m)
    stats_finish(SP_B)
    for m in range(NKC):
        pass2(m, 528, TC)
    h1nbufs = [buf(f"h1n{q}") for q in range(4)]
    for e_ in ("act", "pool", "dve"):
        P.wait(e_, spill_toks)
    WIN = (2, 4, 8, 16)
    zbufs = [buf(f"z{m}") for m in range(NKC)]
    for gi in range(4):
        w = WIN[gi]
        for hb_ in range(2):
            wiu, wbu = get_w()
            for cl2 in range(2):
                cl = hb_ * 2 + cl2
                par = cl % 2
                ub = buf(f"u{par}")
                res = mm_chunk(wiu, wbu, cl2 * 128, h1n, h1nbufs, CP)
                for (bk, c0, n) in res:
                    P.do("act", lambda e, bk=bk, c0=c0, n=n, par=par: e.activation(
                        out=u_sb[par][:, c0:c0 + n], in_=pb[bk][:, :n], func=AF.Copy),
                        reads=[bank[bk]], writes=[ub])
                cur, curb = u_sb[par], ub
                k = 1
                pp = [s2[par], s4[par]]
                pi_ = 0
                while k < w:
                    dstt = pp[pi_ % 2]
                    db = buf(f"s{pi_ % 2}_{par}")
                    P.do("dve", lambda e, cur=cur, dstt=dstt, k=k: e.tensor_tensor(
                        out=dstt[:, k:TC], in0=cur[:, k:TC], in1=cur[:, 0:TC - k], op=ALU.add),
                        reads=[curb], writes=[db])
                    cur, curb = dstt, db
                    k *= 2
                    pi_ += 1
                P.do("dve", lambda e, cur=cur, cl=cl, w=w, par=par: e.scalar_tensor_tensor(
                    out=mixed[:, cl, :], in0=cur[:, 16:TC], scalar=1.0 / w, in1=u_sb[par][:, 16:TC],
                    op0=ALU.mult, op1=ALU.subtract),
                    reads=[curb, ub], writes=[buf("mixed")])
        for half in range(2):
            gw_ = get_w()
            gr_ = get_w()
            wig, wbg = gw_
            wgi, wbgr = gr_
            for d2 in range(2):
                dl = half * 2 + d2
                m = gi * 4 + dl
                i = m % 2
                res = mm_chunk(wig, wbg, d2 * 128, h1n, h1nbufs, CPO)
                for (bk, c0, n) in res:
                    P.do("act", lambda e, bk=bk, c0=c0, n=n, i=i: e.activation(out=sgc[i][:, c0 - 16:c0 - 16 + n],
                                                                               in_=pb[bk][:, :n], func=AF.Silu),
                         reads=[bank[bk]], writes=[buf(f"sgc{i}")])
            for d2 in range(2):
                dl = half * 2 + d2
                m = gi * 4 + dl
                i = m % 2
                res = mm_chunk(wgi, wbgr, d2 * 128, mixed, [buf("mixed")], PO, nk=4)
                for (bk, c0, n) in res:
                    P.do("dve", lambda e, bk=bk, c0=c0, n=n, m=m, i=i: e.scalar_tensor_tensor(
                        out=zT[:, m, c0:c0 + n], in0=pb[bk][:, :n], scalar=gv[:, 2, m:m + 1],
                        in1=sgc[i][:, c0:c0 + n], op0=ALU.mult, op1=ALU.mult),
                        reads=[bank[bk], buf("gv"), buf(f"sgc{i}")], writes=[zbufs[m]])
    SP_OA = [(0, 512, 6)]
    SP_OB = [(512, 512, 7)]
    out_toks = []
    fin = {"ld": 0}

    def c4_chunk(wi, wbf, ml, m, spieces, stream):
        res = mm_chunk(wi, wbf, ml * 128, zT, zbufs, [(c0, n) for (c0, n, _) in spieces])
        for (bk, c0, n) in res:
            P.do("act", lambda e, bk=bk, c0=c0, n=n, m=m: e.activation(out=ybuf[:, m, c0:c0 + n],
                                                                         in_=pb[bk][:, :n], func=AF.Copy),
                 reads=[bank[bk]], writes=[ybufs[m]])
        stats_add(stream, m, ybuf[:, m, :], ybufs[m], spieces, NKC)

    def ld_h1o(m, lo, hi):
        i = fin["ld"] % NS
        fin["ld"] += 1
        P.dma("sp", stg[i][:, lo:hi], h1v[:, m, 16 + lo:16 + hi], f"stg{i}", reads=[h1d[m]], writes=[buf(f"stg{i}")],
              extra=b_toks + spill_toks)
        return i

    def fin_chunk(i, m, lo, hi, k_add):
        sb = buf(f"stg{i}")
        hb = buf(f"h1c{i}")
        P.do("dve", lambda e: e.scalar_tensor_tensor(out=h1c[i][:, lo:hi], in0=ybuf[:, m, lo:hi],
                                                     scalar=gv[:, 3, m:m + 1], in1=rstd[:, lo:hi],
                                                     op0=ALU.mult, op1=ALU.mult),
             reads=[ybufs[m], buf("gv"), buf("rstd")], writes=[hb], extra=b_toks)
        P.do("pool" if k_add % 3 != 2 else "dve", lambda e: e.tensor_tensor(
            out=h1c[i][:, lo:hi], in0=h1c[i][:, lo:hi], in1=stg[i][:, lo:hi], op=ALU.add),
            reads=[sb, hb], writes=[hb])
        out_toks.append(P.dma("sp", outv[:, m, lo:hi], h1c[i][:, lo:hi], f"ost{i}", reads=[hb]))

    for cb_ in range(8):
        wi, wbf = get_w()
        for ml in range(2):
            c4_chunk(wi, wbf, ml, cb_ * 2 + ml, SP_OA, 0)
    stats_finish(SP_OA)
    pre = [ld_h1o(m, 0, 512) for m in range(NS - 1)]
    for cb_ in range(8):
        wi, wbf = get_w()
        for ml in range(2):
            m = cb_ * 2 + ml
            c4_chunk(wi, wbf, ml, m, SP_OB, 1)
            if m + NS - 1 < NKC:
                pre.append(ld_h1o(m + NS - 1, 0, 512))
            fin_chunk(pre.pop(0), m, 0, 512, m)
    stats_finish(SP_OB)
    pre = [ld_h1o(m, 512, 1024) for m in range(NS - 1)]
    for m in range(NKC):
        if m + NS - 1 < NKC:
            pre.append(ld_h1o(m + NS - 1, 512, 1024))
        fin_chunk(pre.pop(0), m, 512, 1024, m)
    return out_toks


ARENA_BYTES = 206 * 1024


def build(mode):
    nc = bass.Bass("TRN2", target_bir_lowering=False)
    P = Prog(nc)
    ar = Arena(nc, ARENA_BYTES, "arena")
    pb = [nc.alloc_psum_tensor(f"pb{i}", [128, 512], F32) for i in range(8)]

    def din(name, shape, dt=F32):
        return nc.dram_tensor(name, shape, dt, kind="ExternalInput").ap()

    toks = []
    if mode in ("A", "F"):
        ioA = {
            "hT": din("hT", [D, L]), "wA": din("wA", [D, 2048]), "gpre0": din("gpre0", [128, NKC]),
            "cosT": din("cosT", [32, L]), "sinT": din("sinT", [32, L]), "lamv": din("lamv", [128, 4]),
            "sublnG": din("sublnG", [128, 256]), "ident": din("ident", [128, 128]), "perm": din("perm", [32, 32]),
        }
    if mode in ("C", "F"):
        ioC = {
            "hTs": din("hTs", [D, TC]), "wo0": din("wo0", [D, D]), "wi1": din("wi1", [D, 2 * D]),
            "wg": din("wg", [4, 512, 512]), "wo1": din("wo1", [D, D]),
            "gpost0": din("gpost0", [128, NKC]), "gpre1": din("gpre1", [128, NKC]),
            "pscale": din("pscale", [128, NKC]), "gpost1": din("gpost1", [128, NKC]),
        }
        outT = nc.dram_tensor("outT", [D, 1024], F32, kind="ExternalOutput").ap()
        h1_scr = nc.dram_tensor("h1scr", [D, TC], F32).ap()
    if mode == "A":
        ogT_d = nc.dram_tensor("ogT", [9, 512, 512], BF16, kind="ExternalOutput").ap()
        toks = build_A(nc, P, ar, pb, ioA, ogT_d)
    elif mode == "C":
        ogT_in = din("ogTs", [D, TC], BF16)
        toks = build_C(nc, P, ar, pb, ioC, ogT_in, h1_scr, outT, [])
    else:
        cins = [nc.dram_tensor(f"cc_in{t}", [512, 512], BF16) for t in range(9)]
        cin_all = None
        couts = nc.dram_tensor("cc_out", [9, D, 512], BF16)
        ccs = nc.alloc_semaphore(name="ccsem")
        ncc = [0]

        def after_store(t, tok):
            P.wait("pool", [tok])

            def cc(e, t=t):
                e.collective_compute("AllGather", ALU.bypass, replica_groups=[[0, 1, 2, 3], [4, 5, 6, 7]],
                                     ins=[cins[t].ap().opt()], outs=[couts.ap()[t].opt()]).then_inc(ccs)
            P.raw("pool", cc)
            ncc[0] += 1

        ws = WStream(P, ar, ioC)
        ogT_view = ar.alloc([NKC, TC], BF16)
        pre_bytes = ar.off
        ar.reset(0)
        og_pre = {"bufs": [Buf(f"ogT{q}") for q in range(4)], "parts": (0, 1)}
        st = {}

        def ld(e):
            st["r2"] = (e.partition_id() % 4) * 2
        P.raw("sp", ld)
        coutv = couts.ap().rearrange("s (c p) t -> p c s t", p=128)

        def og_src(q4, part):
            if part == 0:
                return coutv[:, 4 * q4:4 * q4 + 4, bass.ds(st["r2"], 1), 496:512]
            return coutv[:, 4 * q4:4 * q4 + 4, bass.ds(st["r2"] + part, 1), :]

        def hook(wa_readers, dead_bytes):
            assert pre_bytes <= dead_bytes, (pre_bytes, dead_bytes)
            ws.issue_upto(ws.PREF, wa_readers)
            tc8 = (ccs, ncc[0])
            for q4 in range(4):
                for part in (0, 1):
                    c_lo, c_hi = (0, 16) if part == 0 else (16, 528)
                    P.dma("sp", ogT_view[:, 4 * q4:4 * q4 + 4, c_lo:c_hi], (lambda q4=q4, part=part: og_src(q4, part)),
                          f"ogin{part}_{q4}", writes=[og_pre["bufs"][q4]], extra=list(wa_readers) + [tc8])

        tA = build_A(nc, P, ar, pb, ioA, cins, after_store, after_last_proj=hook)
        tcc = (ccs, ncc[0])
        for e_ in Prog.ENG:
            P.wait(e_, tA)
        ar.reset(0)
        toks = build_C(nc, P, ar, pb, ioC, og_src, h1_scr, outT, [tcc], ws=ws, og_pre=og_pre)
    P.wait("sp", toks)
    P.emit()
    return nc


def _rope_tables():
    pos = np.arange(L, dtype=np.float32)
    inv = (np.float32(500000.0) ** (-np.arange(0, 32, 2, dtype=np.float32) / np.float32(32))).astype(np.float32)
    ang = (pos[:, None] * inv[None, :]).astype(np.float32)
    c = np.cos(ang).astype(np.float32).T
    s = np.sin(ang).astype(np.float32).T
    cosT = np.concatenate([c, c], 0)
    sinT = np.concatenate([-s, s], 0)
    return np.ascontiguousarray(cosT), np.ascontiguousarray(sinT)


def _pc(v):
    return np.ascontiguousarray(v.reshape(NKC, 128).T)


_CACHE = {}


def _get(mode):
    if mode not in _CACHE:
        _CACHE[mode] = build(mode)
    return _CACHE[mode]


def kernel(x, meta_tokens, pre_norm_g, post_norm_g, attn_w_in, attn_w_out,
           attn_lambda_q1, attn_lambda_k1, attn_lambda_q2, attn_lambda_k2, attn_subln_g,
           pool_w_in, pool_w_group, pool_scale, pool_w_out):
    f = np.float32
    x = np.asarray(x, f)
    B = x.shape[0]
    hT = [np.ascontiguousarray(np.concatenate([np.asarray(meta_tokens, f), x[b]], 0).T) for b in range(B)]
    cosT, sinT = _rope_tables()
    w_in = np.asarray(attn_w_in, f)[0]
    ident = np.eye(128, dtype=f)
    perm = np.zeros((32, 32), f)
    for i in range(32):
        perm[(i + 16) % 32, i] = 1.0
    lamv = np.ascontiguousarray(np.stack([np.asarray(a, f)[0] for a in
                                          (attn_lambda_q1, attn_lambda_k1, attn_lambda_q2, attn_lambda_k2)], 1))
    sublnG = np.ascontiguousarray(np.broadcast_to(np.asarray(attn_subln_g, f)[0][None, :], (128, 256)))
    gpre0 = _pc(np.asarray(pre_norm_g, f)[0])
    cm = {"wo0": np.asarray(attn_w_out, f)[0], "wi1": np.asarray(pool_w_in, f)[0],
          "wg": np.asarray(pool_w_group, f)[0], "wo1": np.asarray(pool_w_out, f)[0],
          "gpost0": _pc(np.asarray(post_norm_g, f)[0]), "gpre1": _pc(np.asarray(pre_norm_g, f)[1]),
          "pscale": _pc(np.asarray(pool_scale, f)[0]), "gpost1": _pc(np.asarray(post_norm_g, f)[1])}
    mapsA = []
    for core in range(8):
        b, r = core // 4, core % 4
        cols = np.concatenate([np.arange(q * 2048 + r * 512, q * 2048 + r * 512 + 512) for q in range(4)])
        mapsA.append({"hT": hT[b], "wA": np.ascontiguousarray(w_in[:, cols]), "gpre0": gpre0, "cosT": cosT,
                      "sinT": sinT, "lamv": lamv, "sublnG": sublnG, "ident": ident, "perm": perm})
    out = np.empty((B, SEQ, D), f)
    if FUSED:
        ncF = _get("F")
        in_maps = []
        for core in range(8):
            b, r = core // 4, core % 4
            d = dict(cm)
            d.update(mapsA[core])
            d["hTs"] = np.ascontiguousarray(hT[b][:, 1024 * r:1024 * r + TC])
            in_maps.append(d)
        res = run_bass_kernel_spmd(ncF, in_maps, core_ids=list(range(8)))
        for core in range(8):
            b, r = core // 4, core % 4
            out[b, 1024 * r:1024 * r + 1024, :] = np.asarray(res.results[core]["outT"]).T
        return out
    resA = run_bass_kernel_spmd(_get("A"), mapsA, core_ids=list(range(8)))
    def _asm(a):
        a = np.asarray(a)
        return np.concatenate([a[0][:, 496:512]] + [a[t_] for t_ in range(1, 9)], 1)
    ogT = [np.concatenate([_asm(resA.results[4 * b + r]["ogT"]) for r in range(4)], 0) for b in range(B)]
    in_maps = []
    for core in range(8):
        b, r = core // 4, core % 4
        d = dict(cm)
        d["hTs"] = np.ascontiguousarray(hT[b][:, 1024 * r:1024 * r + TC])
        d["ogTs"] = np.ascontiguousarray(ogT[b][:, 1024 * r:1024 * r + TC])
        in_maps.append(d)
    resC = run_bass_kernel_spmd(_get("C"), in_maps, core_ids=list(range(8)))
    for core in range(8):
        b, r = core // 4, core % 4
        out[b, 1024 * r:1024 * r + 1024, :] = np.asarray(resC.results[core]["outT"]).T
    return out
```

```python
import math
import numpy as np
import concourse.bass as bass
import concourse.mybir as mybir
from concourse.bass_utils import run_bass_kernel_spmd

F32 = mybir.dt.float32
BF16 = mybir.dt.bfloat16
AF = mybir.ActivationFunctionType
ALU = mybir.AluOpType
AX = mybir.AxisListType

D = 2048
SEQ = 4096
NMETA = 16
L = SEQ + NMETA
EPS = 1e-6
NKC = 16
TC = 1040
LAM_INIT0 = 0.8 - 0.6 * math.exp(-0.3 * 0)
SCALE = 128 ** -0.5
FUSED = True
NT_A = 9


class Buf:
    __slots__ = ("name", "w", "r")

    def __init__(self, name):
        self.name = name
        self.w = None
        self.r = []


class Prog:
    ENG = ("pe", "act", "dve", "pool", "sp")

    def __init__(self, nc):
        self.nc = nc
        self.ops = {e: [] for e in self.ENG}
        self.sem = {e: nc.alloc_semaphore(name=f"s_{e}") for e in self.ENG}
        self.cnt = {e: 0 for e in self.ENG}
        self.waited = {e: {} for e in self.ENG}
        self.dma_sems = {}
        self.pend_r = {e: [] for e in self.ENG}
        self.pend_w = {e: [] for e in self.ENG}

    def _waits(self, eng, deps):
        waits = []
        for d in deps:
            if d is None:
                continue
            if d == "pending":
                assert eng == "pe", "dependency on unresolved pending write"
                continue
            s, v = d
            if eng == "pe" and s is self.sem["pe"]:
                continue
            if self.waited[eng].get(s.num, 0) >= v:
                continue
            self.waited[eng][s.num] = v
            waits.append((s, v))
        return waits

    def _deps(self, reads, writes, extra):
        deps = list(extra)
        for b in reads:
            deps.append(b.w)
        for b in writes:
            deps.extend(b.r)
            deps.append(b.w)
        return deps

    def _commit(self, eng, tok, reads, writes):
        for b in self.pend_r[eng]:
            b.r.append(tok)
        for b in self.pend_w[eng]:
            b.w = tok
        self.pend_r[eng] = []
        self.pend_w[eng] = []
        for b in reads:
            b.r.append(tok)
        for b in writes:
            b.w = tok
            b.r = []

    def do(self, eng, fn, reads=(), writes=(), sig=True, extra=()):
        deps = self._deps(reads, writes, extra)
        waits = self._waits(eng, deps)
        tok = None
        if sig:
            self.cnt[eng] += 1
            tok = (self.sem[eng], self.cnt[eng])
        s_own = self.sem[eng]

        def run(e, fn=fn, waits=waits, sig=sig):
            for (s, v) in waits:
                e.wait_ge(s, v)
            ins = fn(e)
            if sig:
                ins.then_inc(s_own, 1)
        self.ops[eng].append(run)
        if sig:
            self._commit(eng, tok, reads, writes)
        else:
            self.pend_r[eng].extend(reads)
            for b in writes:
                b.r = []
                b.w = "pending"
                self.pend_w[eng].append(b)
        return tok

    def dma(self, eng, out, in_, sem_name, reads=(), writes=(), extra=()):
        if sem_name not in self.dma_sems:
            self.dma_sems[sem_name] = [self.nc.alloc_semaphore(name=f"d_{sem_name}"), 0]
        ent = self.dma_sems[sem_name]
        ent[1] += 16
        s, v = ent[0], ent[1]
        deps = self._deps(reads, writes, extra)
        waits = self._waits(eng, deps)

        def run(e, waits=waits):
            for (ws, wv) in waits:
                e.wait_ge(ws, wv)
            src = in_() if callable(in_) else in_
            e.dma_start(out=out, in_=src).then_inc(s, 16)
        self.ops[eng].append(run)
        tok = (s, v)
        for b in reads:
            b.r.append(tok)
        for b in writes:
            b.w = tok
            b.r = []
        return tok

    def wait(self, eng, deps):
        waits = self._waits(eng, deps)

        def run(e, waits=waits):
            for (ws, wv) in waits:
                e.wait_ge(ws, wv)
        self.ops[eng].append(run)

    def raw(self, eng, fn):
        self.ops[eng].append(fn)

    def emit(self):
        nc = self.nc
        with nc.Block() as block:
            @block.tensor
            def _(e):
                for f in self.ops["pe"]:
                    f(e)

            @block.scalar
            def _(e):
                for f in self.ops["act"]:
                    f(e)

            @block.vector
            def _(e):
                for f in self.ops["dve"]:
                    f(e)

            @block.gpsimd
            def _(e):
                for f in self.ops["pool"]:
                    f(e)

            @block.sync
            def _(e):
                for f in self.ops["sp"]:
                    f(e)


class Arena:
    def __init__(self, nc, nbytes, name):
        self.t = nc.alloc_sbuf_tensor(name, [128, nbytes // 4], F32)
        self.cap = nbytes
        self.off = 0

    def reset(self, off=0):
        self.off = off

    def alloc(self, shape, dtype):
        n = 1
        for s in shape:
            n *= s
        sz = 2 if dtype == BF16 else 4
        nb = (n * sz + 31) // 32 * 32
        assert self.off + nb <= self.cap, f"arena overflow {self.off}+{nb}>{self.cap}"
        v = self.t[:, self.off // 4:(self.off + nb) // 4]
        if dtype == BF16:
            v = v.bitcast(BF16)
        v = v[:, 0:n]
        self.off += nb
        if len(shape) == 2:
            return v.rearrange("p (a b) -> p a b", a=shape[0])
        if len(shape) == 3:
            return v.rearrange("p (a b c) -> p a b c", a=shape[0], b=shape[1])
        return v


def build_A(nc, P, ar, pb, io, og_slots, after_store=None, after_last_proj=None):
    hTv = io["hT"].rearrange("(c p) t -> p c t", p=128)
    wAv = io["wA"].rearrange("(c p) n -> p c n", p=128)
    if isinstance(og_slots, list):
        ogv_t = [h.ap().rearrange("(c p) t -> p c t", p=128) for h in og_slots]
    else:
        ogv_t = [og_slots[t_].rearrange("(c p) t -> p c t", p=128) for t_ in range(9)]

    wA = ar.alloc([NKC, 2048], BF16)
    xraw = ar.alloc([4096], F32)
    xb = xraw.bitcast(BF16).rearrange("p (c t) -> p c t", c=NKC)
    sq = ar.alloc([NKC, 512], BF16)
    dead_mark = ar.off
    kT = ar.alloc([4, L], BF16)
    Vp = ar.alloc([33, 2, 257], BF16)
    qT = ar.alloc([4, 512], BF16)
    sg = ar.alloc([4, 512], F32)
    rstd_fm = ar.alloc([512], F32)
    rstd_tm = ar.alloc([4], F32)
    cosb = ar.alloc([512], F32)
    sinb = ar.alloc([512], F32)
    t1 = ar.alloc([512], F32)
    t2 = ar.alloc([512], F32)
    NE = 5
    e_sb = [ar.alloc([512], BF16) for _ in range(NE)]
    o_sb = ar.alloc([4, 256], F32)
    og_sb = ar.alloc([4, 512], BF16)
    ogT_st = ar.alloc([4, 512], BF16)
    ident = ar.alloc([128], BF16)
    perm = ar.alloc([32], BF16)
    onesM = ar.alloc([128], BF16)
    onesF = ar.alloc([128], F32)
    gpre = ar.alloc([NKC], F32)
    lamv = ar.alloc([4], F32)
    lamt = ar.alloc([8], F32)
    sublnG = ar.alloc([256], F32)
    small = ar.alloc([64], F32)
    epsc = ar.alloc([8], F32)
    tmp256 = ar.alloc([256], F32)
    tmp256b = ar.alloc([256], F32)
    wstg = [xraw[:, 0:2048], xraw[:, 2048:4096]]

    B = {}

    def buf(name):
        if name not in B:
            B[name] = Buf(name)
        return B[name]

    bank = [buf(f"bank{i}") for i in range(8)]

    P.dma("sp", gpre, io["gpre0"], "c0", writes=[buf("gpre")])
    P.dma("sp", lamv, io["lamv"], "c1", writes=[buf("lamv")])
    P.dma("sp", sublnG, io["sublnG"], "c2", writes=[buf("sublnG")])
    P.dma("pool", ident, io["ident"], "c3", writes=[buf("ident")])
    P.dma("pool", perm[0:32, :], io["perm"], "c4", writes=[buf("perm")])
    P.do("dve", lambda e: e.memset(onesM, 2.0 ** -11), writes=[buf("onesM")])
    P.do("dve", lambda e: e.memset(onesF, 1.0), writes=[buf("onesF")])
    P.do("pool", lambda e: e.memset(Vp[:, :, :, 256:257], 1.0), writes=[buf("Vp")])
    P.do("dve", lambda e: e.memset(epsc[:, 0:1], EPS), writes=[buf("epsc")])
    P.do("dve", lambda e: e.memset(epsc[:, 1:2], math.log(1.0 - LAM_INIT0)), writes=[buf("epsc")])

    P.do("dve", lambda e: e.tensor_tensor(out=lamt[:, 0:1], in0=lamv[:, 0:1], in1=lamv[:, 1:2], op=ALU.mult),
         reads=[buf("lamv")], writes=[buf("lamt")])
    P.do("dve", lambda e: e.tensor_tensor(out=lamt[:, 1:2], in0=lamv[:, 2:3], in1=lamv[:, 3:4], op=ALU.mult),
         reads=[buf("lamv")], writes=[buf("lamt")])
    P.do("pe", lambda e: e.matmul(pb[7][:, 0:2], lhsT=onesF, rhs=lamt[:, 0:2], start=True, stop=True),
         reads=[buf("onesF"), buf("lamt")], writes=[bank[7]])
    P.do("act", lambda e: e.activation(out=lamt[:, 2:4], in_=pb[7][:, 0:2], func=AF.Exp),
         reads=[bank[7]], writes=[buf("lamt2")])
    P.do("dve", lambda e: e.scalar_tensor_tensor(out=lamt[:, 4:5], in0=lamt[:, 3:4], scalar=-LAM_INIT0,
                                                 in1=lamt[:, 2:3], op0=ALU.add, op1=ALU.subtract),
         reads=[buf("lamt2")], writes=[buf("neglam")])
    neglam = lamt[:, 4:5]

    WT = []
    for kc in range(NKC):
        sb = buf(f"wstg{kc % 2}")
        P.dma("sp", wstg[kc % 2], wAv[:, kc, :], f"wstg{kc % 2}", writes=[sb])
        wt = P.do("dve", lambda e, kc=kc: e.tensor_scalar(out=wA[:, kc, :], in0=wstg[kc % 2],
                                                          scalar1=gpre[:, kc:kc + 1], scalar2=None, op0=ALU.mult),
                  reads=[sb, buf("gpre")], writes=[buf("wA")])
        WT.append(wt)

    tiles = [(0, 16)] + [(16 + 512 * i, 512) for i in range(8)]

    def tile_blocks(t):
        if t == 0:
            return [(0, 0, 16)]
        return [(4 * (t - 1) + 1 + j, 128 * j, 128) for j in range(4)]

    wtoks = WT[-2:]

    def prep_load(t):
        pos0, NT = tiles[t]
        for q4 in range(4):
            P.dma("pool", xb[:, 4 * q4:4 * q4 + 4, :NT], hTv[:, 4 * q4:4 * q4 + 4, pos0:pos0 + NT], f"xb{q4}",
                  writes=[buf(f"xb{q4}")], extra=(wtoks if t == 0 else ()))
        P.dma("sp", cosb[0:32, :NT], io["cosT"][:, pos0:pos0 + NT], "cos", writes=[buf("cos")])
        P.dma("sp", sinb[0:32, :NT], io["sinT"][:, pos0:pos0 + NT], "sin", writes=[buf("sin")])

    def prep_squares(t):
        pos0, NT = tiles[t]
        for q4 in range(4):
            P.do("pool", lambda e, q4=q4: e.tensor_tensor(out=sq[:, 4 * q4:4 * q4 + 4, :NT],
                                                          in0=xb[:, 4 * q4:4 * q4 + 4, :NT],
                                                          in1=xb[:, 4 * q4:4 * q4 + 4, :NT], op=ALU.mult),
                 reads=[buf(f"xb{q4}")], writes=[buf(f"sq{q4}")])

    def prep_stats(t):
        pos0, NT = tiles[t]
        blks = tile_blocks(t)
        for q4 in range(4):
            sqb = buf(f"sq{q4}")
            for k4 in range(4):
                kc = 4 * q4 + k4
                P.do("pe", lambda e, kc=kc: e.matmul(pb[6][:, :NT], lhsT=onesM, rhs=sq[:, kc, :NT],
                                                     start=(kc == 0), stop=(kc == NKC - 1)),
                     reads=[sqb, buf("onesM")], writes=[bank[6]], sig=False)
                for bi, (kb, off, ntok) in enumerate(blks):
                    last = (bi == len(blks) - 1 and k4 == 3)
                    P.do("pe", lambda e, kc=kc, bi=bi, off=off, ntok=ntok: e.matmul(
                        pb[7][:ntok, bi:bi + 1], lhsT=sq[:, kc, off:off + ntok], rhs=onesM[:, 0:1],
                        start=(kc == 0 and bi == 0), stop=(kc == NKC - 1), skip_group_check=True),
                        reads=[sqb, buf("onesM")], writes=[bank[7]], sig=last)
        P.do("act", lambda e: e.activation(out=rstd_fm[:, :NT], in_=pb[6][:, :NT], func=AF.Ln, bias=epsc[:, 0:1]),
             reads=[bank[6], buf("epsc")], writes=[buf("rstd_fm")])
        P.do("act", lambda e: e.activation(out=rstd_fm[:, :NT], in_=rstd_fm[:, :NT], func=AF.Exp, scale=-0.5),
             reads=[buf("rstd_fm")], writes=[buf("rstd_fm")])
        nb = len(blks)
        np_ = blks[0][2]
        P.do("act", lambda e: e.activation(out=rstd_tm[:np_, :nb], in_=pb[7][:np_, :nb], func=AF.Ln,
                                           bias=epsc[:np_, 0:1]),
             reads=[bank[7], buf("epsc")], writes=[buf("rstd_tm")])
        P.do("act", lambda e: e.activation(out=rstd_tm[:np_, :nb], in_=rstd_tm[:np_, :nb], func=AF.Exp, scale=-0.5),
             reads=[buf("rstd_tm")], writes=[buf("rstd_tm")])

    pj = [0]

    def proj(t):
        pos0, NT = tiles[t]
        pj[0] = 0
        xbufs = [buf(f"xb{q}") for q in range(4)]
        ropes = []

        def do_rope(dst, dbuf):
            d32 = dst[0:32, :]
            P.do("pe", lambda e, d32=d32: e.matmul(pb[7][0:32, :NT], lhsT=perm[0:32, 0:32], rhs=d32,
                                                   start=True, stop=True),
                 reads=[dbuf, buf("perm")], writes=[bank[7]])
            P.do("pool", lambda e, d32=d32: e.tensor_tensor(out=t1[0:32, :NT], in0=d32, in1=cosb[0:32, :NT],
                                                            op=ALU.mult),
                 reads=[dbuf, buf("cos")], writes=[buf("t1")])
            P.do("dve", lambda e: e.tensor_tensor(out=t2[0:32, :NT], in0=pb[7][0:32, :NT], in1=sinb[0:32, :NT],
                                                  op=ALU.mult),
                 reads=[bank[7], buf("sin")], writes=[buf("t2")])
            P.do("pool", lambda e, d32=d32: e.tensor_tensor(out=d32, in0=t1[0:32, :NT], in1=t2[0:32, :NT],
                                                            op=ALU.add),
                 reads=[buf("t1"), buf("t2")], writes=[dbuf])

        for oc in range(8):
            bk = pj[0] % 4
            pj[0] += 1
            for kc in range(NKC):
                P.do("pe", lambda e, kc=kc, oc=oc, bk=bk: e.matmul(
                    pb[bk][:, :NT], lhsT=wA[:, kc, oc * 128:(oc + 1) * 128], rhs=xb[:, kc, :NT],
                    start=(kc == 0), stop=(kc == NKC - 1)),
                    reads=[buf("wA"), xbufs[kc // 4]], writes=[bank[bk]], sig=(kc == NKC - 1))
            if oc < 4:
                dst = qT[:, oc, :NT]
                dbuf = buf(f"qT{oc}")
            else:
                dst = kT[:, oc - 4, pos0:pos0 + NT]
                dbuf = buf(f"kT{oc - 4}")
            P.do("dve", lambda e, dst=dst, bk=bk: e.tensor_tensor(out=dst, in0=pb[bk][:, :NT], in1=rstd_fm[:, :NT],
                                                                  op=ALU.mult),
                 reads=[bank[bk], buf("rstd_fm")], writes=[dbuf])
            ropes.append((dst, dbuf))
            if len(ropes) > 1:
                do_rope(*ropes.pop(0))
            if oc == 0:
                advance_subln(1)
            if oc == 2:
                advance_subln(2)
            if oc == 6:
                flush_deferred()
        blks = tile_blocks(t)
        for bi, (kb, off, ntok) in enumerate(blks):
            for which in range(2):
                bk = pj[0] % 4
                pj[0] += 1
                c0 = 1024 + 512 * which
                for kc in range(NKC):
                    P.do("pe", lambda e, kc=kc, bk=bk, off=off, ntok=ntok, c0=c0: e.matmul(
                        pb[bk][:ntok, 0:512], lhsT=xb[:, kc, off:off + ntok], rhs=wA[:, kc, c0:c0 + 512],
                        start=(kc == 0), stop=(kc == NKC - 1)),
                        reads=[buf("wA"), xbufs[kc // 4]], writes=[bank[bk]], sig=(kc == NKC - 1))
                if which == 0:
                    P.do("act", lambda e, bk=bk, kb=kb, bi=bi, ntok=ntok: e.activation(
                        out=Vp[:ntok, kb, :, 0:256], in_=pb[bk][:ntok, 0:512].rearrange("p (h d) -> p h d", h=2),
                        func=AF.Copy, scale=rstd_tm[:ntok, bi:bi + 1]),
                        reads=[bank[bk], buf("rstd_tm")], writes=[buf("Vp")])
                else:
                    P.do("act", lambda e, bk=bk, bi=bi, ntok=ntok: e.activation(
                        out=sg[:ntok, bi, :], in_=pb[bk][:ntok, 0:512], func=AF.Silu,
                        scale=rstd_tm[:ntok, bi:bi + 1]),
                        reads=[bank[bk], buf("rstd_tm")], writes=[buf("sg")])
            if bi == 0:
                while ropes:
                    do_rope(*ropes.pop(0))

    deferred = []
    pending_subln = []

    def advance_subln(upto):
        for d in pending_subln:
            if d["done"] < 1 and upto >= 1:
                d["A"]()
                d["done"] = 1
            if d["done"] < 2 and upto >= 2:
                d["B"]()
                d["done"] = 2
        pending_subln[:] = [d for d in pending_subln if d["done"] < 2]

    def flush_deferred():
        advance_subln(2)
        while deferred:
            deferred.pop(0)()

    sc = [0]
    ec = [0]

    def attn_head(t, h):
        pos0, NT = tiles[t]
        blks = tile_blocks(t)
        nqb = len(blks)
        nq = blks[0][2]
        keys = [(0, 0, 16, 0)]
        if t > 0:
            for kb in range(1, 4 * (t - 1) + 1):
                keys.append((kb, 16 + 128 * (kb - 1), 128, 0))
            for j in range(4):
                kb = 4 * (t - 1) + 1 + j
                keys.append((kb, 16 + 128 * (kb - 1), 128, j))
        LOOK = 3
        items = [(m, ki) + keys[ki] for m in range(2) for ki in range(len(keys))]

        def emit_S(it):
            m, ki, kb, kpos, nk, jv = it
            qi = 2 * h + m
            c0 = jv * 128
            ncols = NT - c0
            sbk = 4 + (sc[0] % 4)
            sc[0] += 1
            ei = ec[0] % NE
            ec[0] += 1
            eb = buf(f"e{ei}")
            P.do("pe", lambda e, sbk=sbk, qi=qi, kpos=kpos, nk=nk, c0=c0, ncols=ncols: e.matmul(
                pb[sbk][:nk, :ncols], lhsT=kT[:, qi, kpos:kpos + nk], rhs=qT[:, qi, c0:c0 + ncols],
                start=True, stop=True),
                reads=[buf(f"kT{qi}"), buf(f"qT{qi}")], writes=[bank[sbk]])
            P.do("act", lambda e, sbk=sbk, ei=ei, nk=nk, ncols=ncols: e.activation(
                out=e_sb[ei][:nk, :ncols], in_=pb[sbk][:nk, :ncols], func=AF.Exp, scale=SCALE),
                reads=[bank[sbk]], writes=[eb])
            if t > 0 and kb >= 4 * (t - 1) + 1:
                P.do("dve", lambda e, ei=ei: e.memset(e_sb[ei][64:128, 0:64], 0.0), writes=[eb])
            return ei

        def emit_PV(it, ei):
            m, ki, kb, kpos, nk, jv = it
            eb = buf(f"e{ei}")
            for j in range(jv, nqb):
                lc = (j - jv) * 128 if t > 0 else 0
                first = (ki == 0)
                last = (kb == blks[j][0])
                lastpv = (j == nqb - 1)
                P.do("pe", lambda e, ei=ei, nk=nk, lc=lc, j=j, kb=kb, first=first, last=last: e.matmul(
                    pb[j][:nq, 0:257], lhsT=e_sb[ei][:nk, lc:lc + nq], rhs=Vp[:nk, kb, h, :],
                    start=first, stop=last),
                    reads=[eb, buf("Vp")], writes=[bank[j]], sig=(last or lastpv))
                if last:
                    evac(m, j)

        def evac(m, j):
            if j == 0:
                advance_subln(2)
            if True:
                sm = small[:nq, 4 * j:4 * j + 4]
                sbf = buf(f"small{j}")
                P.do("dve", lambda e, j=j, sm=sm: e.reciprocal(out=sm[:, 0:1], in_=pb[j][:nq, 256:257]),
                     reads=[bank[j]], writes=[sbf])
                if m == 0:
                    P.do("dve", lambda e, j=j, sm=sm: e.tensor_scalar(out=o_sb[:nq, j, :], in0=pb[j][:nq, 0:256],
                                                                      scalar1=sm[:, 0:1], scalar2=None, op0=ALU.mult),
                         reads=[bank[j], sbf], writes=[buf(f"o{j}")])
                else:
                    P.do("dve", lambda e, sm=sm: e.tensor_tensor(out=sm[:, 1:2], in0=sm[:, 0:1], in1=neglam[:nq, :],
                                                                 op=ALU.mult),
                         reads=[sbf, buf("neglam")], writes=[sbf])
                    P.do("dve", lambda e, j=j, sm=sm: e.scalar_tensor_tensor(
                        out=o_sb[:nq, j, :], in0=pb[j][:nq, 0:256], scalar=sm[:, 1:2], in1=o_sb[:nq, j, :],
                        op0=ALU.mult, op1=ALU.add),
                        reads=[bank[j], sbf, buf(f"o{j}")], writes=[buf(f"o{j}")])

        pend = []
        for n_it, it in enumerate(items):
            ei = emit_S(it)
            pend.append((it, ei))
            if len(pend) > LOOK:
                emit_PV(*pend.pop(0))
            if n_it == 3:
                advance_subln(1)
            if n_it == 9:
                advance_subln(2)
        while pend:
            emit_PV(*pend.pop(0))
        pending_subln.append({"A": lambda: subln_A(t, h), "B": lambda: subln_B(t, h), "done": 0})

    def subln_A(t, h):
        blks = tile_blocks(t)
        nq = blks[0][2]
        for j in range(len(blks)):
            sm = small[:nq, 16 + 4 * j:16 + 4 * j + 4]
            P.do("pool", lambda e, j=j: e.tensor_tensor(out=tmp256[:nq, :], in0=o_sb[:nq, j, :], in1=o_sb[:nq, j, :],
                                                        op=ALU.mult),
                 reads=[buf(f"o{j}")], writes=[buf("tmp256")])
            P.do("dve", lambda e, sm=sm: e.tensor_reduce(out=sm[:, 0:1], in_=tmp256[:nq, :], axis=AX.X, op=ALU.add),
                 reads=[buf("tmp256")], writes=[buf(f"small2{j}")])

    def subln_B(t, h):
        blks = tile_blocks(t)
        nq = blks[0][2]
        for j in range(len(blks)):
            sm = small[:nq, 16 + 4 * j:16 + 4 * j + 4]
            sbf = buf(f"small2{j}")
            P.do("act", lambda e, sm=sm: e.activation(out=sm[:, 1:2], in_=sm[:, 0:1], func=AF.Ln, scale=1.0 / 256,
                                                      bias=epsc[:nq, 0:1]),
                 reads=[sbf, buf("epsc")], writes=[sbf])
            P.do("act", lambda e, sm=sm: e.activation(out=sm[:, 2:3], in_=sm[:, 1:2], func=AF.Exp, scale=-0.5,
                                                      bias=epsc[:nq, 1:2]),
                 reads=[sbf, buf("epsc")], writes=[sbf])
        for j in range(len(blks)):
            sm = small[:nq, 16 + 4 * j:16 + 4 * j + 4]
            sbf = buf(f"small2{j}")
            P.do("dve", lambda e, j=j, sm=sm: e.scalar_tensor_tensor(
                out=tmp256b[:nq, :], in0=o_sb[:nq, j, :], scalar=sm[:, 2:3], in1=sublnG[:nq, :],
                op0=ALU.mult, op1=ALU.mult),
                reads=[buf(f"o{j}"), sbf, buf("sublnG")], writes=[buf("tmp256b")])
            P.do("pool", lambda e, j=j: e.tensor_tensor(out=og_sb[:nq, j, h * 256:(h + 1) * 256], in0=tmp256b[:nq, :],
                                                        in1=sg[:nq, j, h * 256:(h + 1) * 256], op=ALU.mult),
                 reads=[buf("tmp256b"), buf("sg")], writes=[buf(f"og{j}")])

    def finish_tile(t):
        pos0, NT = tiles[t]
        blks = tile_blocks(t)
        nq = blks[0][2]
        for j in range(len(blks)):
            tb = 4 + (j % 2)
            pbT = pb[tb][:, :].bitcast(BF16)
            for c in range(4):
                P.do("pe", lambda e, j=j, c=c, pbT=pbT: e.transpose(pbT[:, c * 128:c * 128 + nq],
                                                                    og_sb[:nq, j, c * 128:(c + 1) * 128],
                                                                    ident[:nq, :nq]),
                     reads=[buf(f"og{j}"), buf("ident")], writes=[bank[tb]], sig=(c == 3))
            P.do("dve", lambda e, j=j, pbT=pbT: e.tensor_copy(
                out=ogT_st[:, :, j * 128:j * 128 + nq],
                in_=pbT[:, 0:512].rearrange("p (c q) -> p c q", c=4)[:, :, 0:nq]),
                reads=[bank[tb]], writes=[buf("ogT_st")])
        dst = ogv_t[t][:, :, 512 - NT:512]
        tok = P.dma("sp", dst, ogT_st[:, :, :NT], "ogst", reads=[buf("ogT_st")])
        if after_store is not None:
            after_store(t, tok)
        return tok

    P.do("dve", lambda e: e.memset(ogT_st, 0.0), writes=[buf("ogT_st")])
    P.dma("sp", ogv_t[0][:, :, 0:496], ogT_st[:, :, 0:496], "ogz", reads=[buf("ogT_st")])
    out_toks = []
    prep_load(0)
    prep_squares(0)
    prep_stats(0)
    for t in range(NT_A):
        proj(t)
        if t + 1 < NT_A:
            prep_load(t + 1)
            prep_squares(t + 1)
        elif after_last_proj is not None:
            assert not P.pend_r["pe"]
            deadb = [buf("wA")] + [buf(f"xb{q}") for q in range(4)] + [buf(f"sq{q}") for q in range(4)]
            toks_ = []
            for b_ in deadb:
                toks_.extend(b_.r)
                toks_.append(b_.w)
            after_last_proj(toks_, dead_mark)
        attn_head(t, 0)
        if t + 1 < NT_A:
            prep_stats(t + 1)
        attn_head(t, 1)
        deferred.append(lambda t=t: out_toks.append(finish_tile(t)))
    flush_deferred()
    return out_toks


class WStream:
    NW = 4
    PREF = 3

    def __init__(self, P, ar, io):
        self.P = P
        self.wbuf = [ar.alloc([NKC, 256], BF16) for _ in range(self.NW)]
        wo0 = io["wo0"].rearrange("(c p) n -> p c n", p=128)
        wi1 = io["wi1"].rearrange("(c p) n -> p c n", p=128)
        wgv = io["wg"].rearrange("g (c p) n -> p g c n", p=128)
        wo1 = io["wo1"].rearrange("(c p) n -> p c n", p=128)
        wl = []
        for rep in range(2):
            for cb_ in range(8):
                wl.append(wo0[:, :, cb_ * 256:(cb_ + 1) * 256])
        for gi in range(4):
            for hb in range(2):
                wl.append(wi1[:, :, gi * 512 + hb * 256:gi * 512 + (hb + 1) * 256])
            for hb in range(2):
                wl.append(wi1[:, :, 2048 + gi * 512 + hb * 256:2048 + gi * 512 + (hb + 1) * 256])
                wl.append(wgv[:, gi, :, hb * 256:(hb + 1) * 256])
        for rep in range(2):
            for cb_ in range(8):
                wl.append(wo1[:, :, cb_ * 256:(cb_ + 1) * 256])
        self.wlist = wl
        self.bufs = [Buf(f"w{i}") for i in range(self.NW)]
        self.issued = 0
        self.next = 0

    def issue_upto(self, n, extra=()):
        while self.issued < min(len(self.wlist), n):
            idx = self.issued
            v = self.wlist[idx]
            i = idx % self.NW
            self.P.dma("pool", self.wbuf[i][:, :v.shape[1], :], v, f"w{i}", writes=[self.bufs[i]], extra=extra)
            self.issued += 1

    def get(self):
        idx = self.next
        self.next += 1
        self.issue_upto(idx + self.PREF)
        return idx % self.NW, self.bufs[idx % self.NW]


def build_C(nc, P, ar, pb, io, ogT_src, h1_scr, out_dst, deps_in, ws=None, og_pre=None):
    if callable(ogT_src):
        og_src = ogT_src
    else:
        ogv = ogT_src.rearrange("(c p) t -> p c t", p=128)

        def og_src(q4, part):
            if part == 0:
                return ogv[:, 4 * q4:4 * q4 + 4, 0:16]
            return ogv[:, 4 * q4:4 * q4 + 4, 16 + 512 * (part - 1):16 + 512 * part]
    h0v = io["hTs"].rearrange("(c p) t -> p c t", p=128)
    h1v = h1_scr.rearrange("(c p) t -> p c t", p=128)
    outv = out_dst.rearrange("(c p) t -> p c t", p=128)
    CP = [(0, 512), (512, 512), (1024, 16)]
    CPO = [(16, 512), (528, 512)]
    PO = [(0, 512), (512, 512)]

    B = {}

    def buf(name):
        if name not in B:
            B[name] = Buf(name)
        return B[name]

    bank = [buf(f"cbank{i}") for i in range(8)]
    if ws is None:
        ws = WStream(P, ar, io)
    else:
        ar.reset(ws.NW * NKC * 256 * 2)
    wbuf = ws.wbuf
    get_w = ws.get
    ogT = ar.alloc([NKC, TC], BF16)
    zT = ogT
    ymark = ar.off
    ybuf = ar.alloc([NKC, TC], F32)
    yend = ar.off
    h1n = ar.alloc([NKC, TC], BF16)
    NS = 3
    sqc = [ar.alloc([TC], BF16) for _ in range(4)]
    stg = [ar.alloc([TC], F32) for _ in range(NS)]
    h1c = [ar.alloc([TC], F32) for _ in range(NS)]
    rstd = ar.alloc([TC], F32)
    onesM = ar.alloc([128], BF16)
    gv = ar.alloc([4, NKC], F32)
    epsc = ar.alloc([8], F32)
    endmark = ar.off
    ar.reset(ymark)
    u_sb = [ar.alloc([TC], F32) for _ in range(2)]
    s2 = [ar.alloc([TC], F32) for _ in range(2)]
    s4 = [ar.alloc([TC], F32) for _ in range(2)]
    mixed = ar.alloc([4, 1024], BF16)
    sgc = [ar.alloc([1024], F32) for _ in range(2)]
    assert ar.off <= yend
    ar.reset(endmark)

    P.do("dve", lambda e: e.memset(onesM, 2.0 ** -11), writes=[buf("onesM")])
    P.do("dve", lambda e: e.memset(epsc[:, 0:1], EPS), writes=[buf("epsc")])
    for i, nm in enumerate(["gpost0", "gpre1", "pscale", "gpost1"]):
        P.dma("sp", gv[:, i, :], io[nm], f"cg{i}", writes=[buf("gv")])
    if og_pre is not None:
        for q4 in range(4):
            B[f"ogT{q4}"] = og_pre["bufs"][q4]
    for q4 in range(4):
        for part in (0, 1, 2):
            if og_pre is not None and part in og_pre["parts"]:
                continue
            c_lo, c_hi = (0, 16) if part == 0 else (16 + 512 * (part - 1), 16 + 512 * part)
            P.dma("sp", ogT[:, 4 * q4:4 * q4 + 4, c_lo:c_hi], (lambda q4=q4, part=part: og_src(q4, part)),
                  f"ogin{part}_{q4}", writes=[buf(f"ogTB{q4}" if part == 2 else f"ogT{q4}")], extra=deps_in)

    cb = [0]

    def next_bank():
        b = cb[0] % 5
        cb[0] += 1
        return b

    def mm_chunk(wi, wbf, wcol, act, act_bufs, pieces, nk=NKC):
        res = []
        for (c0, n) in pieces:
            bk = next_bank()
            for kc in range(nk):
                P.do("pe", lambda e, kc=kc, bk=bk, c0=c0, n=n: e.matmul(
                    pb[bk][:, :n], lhsT=wbuf[wi][:, kc, wcol:wcol + 128], rhs=act[:, kc, c0:c0 + n],
                    start=(kc == 0), stop=(kc == nk - 1)),
                    reads=[wbf] + act_bufs, writes=[bank[bk]], sig=(kc == nk - 1))
            res.append((bk, c0, n))
        return res

    def stats_add(stream, k, src, src_buf, pieces, total):
        i = 2 * stream + (k % 2)
        sb = buf(f"sqc{i}")
        lo = pieces[0][0]
        hi = pieces[-1][0] + pieces[-1][1]
        while len(stat_pend) > stat_depth[0]:
            stat_pend.pop(0)()
        P.do("act", lambda e, i=i: e.activation(out=sqc[i][:, lo:hi], in_=src[:, lo:hi], func=AF.Square),
             reads=[src_buf], writes=[sb])

        def pe_part():
            for pi, (c0, n, sbk) in enumerate(pieces):
                P.do("pe", lambda e, i=i, c0=c0, n=n, k=k, sbk=sbk: e.matmul(
                    pb[sbk][:, :n], lhsT=onesM, rhs=sqc[i][:, c0:c0 + n], start=(k == 0), stop=(k == total - 1),
                    skip_group_check=True),
                    reads=[sb, buf("onesM")], writes=[bank[sbk]], sig=(pi == len(pieces) - 1))
        stat_pend.append(pe_part)

    stat_pend = []
    stat_depth = [1]

    def stats_finish(pieces):
        while stat_pend:
            stat_pend.pop(0)()
        for (c0, n, sbk) in pieces:
            P.do("act", lambda e, sbk=sbk, c0=c0, n=n: e.activation(out=rstd[:, c0:c0 + n], in_=pb[sbk][:, :n],
                                                                    func=AF.Ln, bias=epsc[:, 0:1]),
                 reads=[bank[sbk], buf("epsc")], writes=[buf("rstd")])
            P.do("act", lambda e, c0=c0, n=n: e.activation(out=rstd[:, c0:c0 + n], in_=rstd[:, c0:c0 + n],
                                                           func=AF.Exp, scale=-0.5),
                 reads=[buf("rstd")], writes=[buf("rstd")])

    SP_A = [(0, 16, 5), (16, 512, 6)]
    SP_B = [(528, 512, 7)]
    SP_ALL = [(0, 512, 6), (512, 512, 7), (1024, 16, 5)]
    SP_O = [(0, 512, 6), (512, 512, 7)]
    ogbufs_s = [[buf(f"ogT{q}") for q in range(4)], [buf(f"ogTB{q}") for q in range(4)]]
    ybufs = [buf(f"y{m}") for m in range(NKC)]
    h1d = [buf(f"h1d{m}") for m in range(NKC)]

    def c1_chunk(wi, wbf, ml, m, spieces, stream):
        res = mm_chunk(wi, wbf, ml * 128, ogT, ogbufs_s[stream], [(c0, n) for (c0, n, _) in spieces])
        for (bk, c0, n) in res:
            P.do("act", lambda e, bk=bk, c0=c0, n=n, m=m: e.activation(out=ybuf[:, m, c0:c0 + n],
                                                                         in_=pb[bk][:, :n], func=AF.Copy),
                 reads=[bank[bk]], writes=[ybufs[m]])
        stats_add(stream, m, ybuf[:, m, :], ybufs[m], spieces, NKC)

    ldc = [0]
    spill_toks = []
    last_spill = {}
    a_toks = []

    def ld_h0(m, lo, hi):
        i = ldc[0] % NS
        ldc[0] += 1
        P.dma("sp", stg[i][:, lo:hi], h0v[:, m, lo:hi], f"stg{i}", writes=[buf(f"stg{i}")])
        return i

    def c2_chunk(i, m, lo, hi, spieces, stream, k_add):
        sb = buf(f"stg{i}")
        hb = buf(f"h1c{i}")
        P.do("dve", lambda e: e.scalar_tensor_tensor(out=h1c[i][:, lo:hi], in0=ybuf[:, m, lo:hi],
                                                     scalar=gv[:, 0, m:m + 1], in1=rstd[:, lo:hi],
                                                     op0=ALU.mult, op1=ALU.mult),
             reads=[ybufs[m], buf("gv"), buf("rstd")], writes=[hb])
        a_toks.append(P.do("pool" if k_add % 3 != 2 else "dve", lambda e: e.tensor_tensor(
            out=ybuf[:, m, lo:hi], in0=h1c[i][:, lo:hi], in1=stg[i][:, lo:hi], op=ALU.add),
            reads=[sb, hb], writes=[ybufs[m]]))
        spill_toks.append(P.dma("sp", h1v[:, m, lo:hi], ybuf[:, m, lo:hi], f"h1st{m % 4}", reads=[ybufs[m]],
                                writes=[h1d[m]], extra=last_spill.get(m % 4, [])))
        last_spill[m % 4] = [spill_toks[-1]]
        stats_add(stream, m, ybuf[:, m, :], ybufs[m], spieces, NKC)

    for cb_ in range(8):
        wi, wbf = get_w()
        for ml in range(2):
            c1_chunk(wi, wbf, ml, cb_ * 2 + ml, SP_A, 0)
    stats_finish(SP_A)
    def pass2(m, lo, hi):
        P.do("dve", lambda e: e.scalar_tensor_tensor(
            out=h1n[:, m, lo:hi], in0=ybuf[:, m, lo:hi], scalar=gv[:, 1, m:m + 1], in1=rstd[:, lo:hi],
            op0=ALU.mult, op1=ALU.mult),
            reads=[ybufs[m], buf("gv"), buf("rstd")], writes=[buf(f"h1n{m // 4}")])

    pre = [ld_h0(m, 0, 528) for m in range(NS - 1)]
    ca = 0
    cp2 = 0
    stat_depth[0] = 2
    for cb_ in range(8):
        wi, wbf = get_w()
        for ml in range(2):
            m = cb_ * 2 + ml
            c1_chunk(wi, wbf, ml, m, SP_B, 1)
            for _ in range(2):
                if ca < NKC:
                    if ca + NS - 1 < NKC:
                        pre.append(ld_h0(ca + NS - 1, 0, 528))
                    c2_chunk(pre.pop(0), ca, 0, 528, SP_A, 0, ca)
                    ca += 1
            if m == 7:
                stat_depth[0] = 1
            if m == 9:
                stats_finish(SP_A)
            if m >= 10:
                for _ in range(3):
                    if cp2 < NKC:
                        pass2(cp2, 0, 528)
                        cp2 += 1
    assert ca == NKC and cp2 == NKC
    stats_finish(SP_B)
    NB6 = 2 * NS
    stgB = [stg[k // 2][:, (k % 2) * 520:(k % 2) * 520 + 512] for k in range(NB6)]
    h1cB = [h1c[k // 2][:, (k % 2) * 520:(k % 2) * 520 + 512] for k in range(NB6)]
    a_half = list(a_toks)
    b_toks = []

    def ld_h0B(m):
        k = m % NB6
        P.dma("sp", stgB[k], h0v[:, m, 528:TC], f"stgB{k}", writes=[buf(f"stgB{k}")], extra=a_half)

    def c2_chunk_B(m):
        k = m % NB6
        sb = buf(f"stgB{k}")
        hb = buf(f"h1cB{k}")
        P.do("dve", lambda e: e.scalar_tensor_tensor(out=h1cB[k], in0=ybuf[:, m, 528:TC], scalar=gv[:, 0, m:m + 1],
                                                     in1=rstd[:, 528:TC], op0=ALU.mult, op1=ALU.mult),
             reads=[ybufs[m], buf("gv"), buf("rstd")], writes=[hb], extra=a_half)
        b_toks.append(P.do("pool" if m % 3 != 2 else "dve", lambda e: e.tensor_tensor(
            out=ybuf[:, m, 528:TC], in0=h1cB[k], in1=stgB[k], op=ALU.add),
            reads=[sb, hb], writes=[ybufs[m]]))
        spill_toks.append(P.dma("sp", h1v[:, m, 528:TC], ybuf[:, m, 528:TC], f"h1st{m % 4}", reads=[ybufs[m]],
                                writes=[h1d[m]], extra=last_spill.get(m % 4, [])))
        last_spill[m % 4] = [spill_toks[-1]]
        stats_add(1, m, ybuf[:, m, :], ybufs[m], SP_B, NKC)

    for m in range(NB6 - 1):
        ld_h0B(m)
    for m in range(NKC):
        if m + NB6 - 1 < NKC:
            ld_h0B(m + NB6 - 1)
        c2_chunk_B(m)
    stats_finish(SP_B)
    for m in range(NKC):
        pass2(m, 528, TC)
    h1nbufs = [buf(f"h1n{q}") for q in range(4)]
    for e_ in ("act", "pool", "dve"):
        P.wait(e_, spill_toks)
    WIN = (2, 4, 8, 16)
    zbufs = [buf(f"z{m}") for m in range(NKC)]
    for gi in range(4):
        w = WIN[gi]
        for hb_ in range(2):
            wiu, wbu = get_w()
            for cl2 in range(2):
                cl = hb_ * 2 + cl2
                par = cl % 2
                ub = buf(f"u{par}")
                res = mm_chunk(wiu, wbu, cl2 * 128, h1n, h1nbufs, CP)
                for (bk, c0, n) in res:
                    P.do("act", lambda e, bk=bk, c0=c0, n=n, par=par: e.activation(
                        out=u_sb[par][:, c0:c0 + n], in_=pb[bk][:, :n], func=AF.Copy),
                        reads=[bank[bk]], writes=[ub])
                cur, curb = u_sb[par], ub
                k = 1
                pp = [s2[par], s4[par]]
                pi_ = 0
                while k < w:
                    dstt = pp[pi_ % 2]
                    db = buf(f"s{pi_ % 2}_{par}")
                    P.do("dve", lambda e, cur=cur, dstt=dstt, k=k: e.tensor_tensor(
                        out=dstt[:, k:TC], in0=cur[:, k:TC], in1=cur[:, 0:TC - k], op=ALU.add),
                        reads=[curb], writes=[db])
                    cur, curb = dstt, db
                    k *= 2
                    pi_ += 1
                P.do("dve", lambda e, cur=cur, cl=cl, w=w, par=par: e.scalar_tensor_tensor(
                    out=mixed[:, cl, :], in0=cur[:, 16:TC], scalar=1.0 / w, in1=u_sb[par][:, 16:TC],
                    op0=ALU.mult, op1=ALU.subtract),
                    reads=[curb, ub], writes=[buf("mixed")])
        for half in range(2):
            gw_ = get_w()
            gr_ = get_w()
            wig, wbg = gw_
            wgi, wbgr = gr_
            for d2 in range(2):
                dl = half * 2 + d2
                m = gi * 4 + dl
                i = m % 2
                res = mm_chunk(wig, wbg, d2 * 128, h1n, h1nbufs, CPO)
                for (bk, c0, n) in res:
                    P.do("act", lambda e, bk=bk, c0=c0, n=n, i=i: e.activation(out=sgc[i][:, c0 - 16:c0 - 16 + n],
                                                                               in_=pb[bk][:, :n], func=AF.Silu),
                         reads=[bank[bk]], writes=[buf(f"sgc{i}")])
            for d2 in range(2):
                dl = half * 2 + d2
                m = gi * 4 + dl
                i = m % 2
                res = mm_chunk(wgi, wbgr, d2 * 128, mixed, [buf("mixed")], PO, nk=4)
                for (bk, c0, n) in res:
                    P.do("dve", lambda e, bk=bk, c0=c0, n=n, m=m, i=i: e.scalar_tensor_tensor(
                        out=zT[:, m, c0:c0 + n], in0=pb[bk][:, :n], scalar=gv[:, 2, m:m + 1],
                        in1=sgc[i][:, c0:c0 + n], op0=ALU.mult, op1=ALU.mult),
                        reads=[bank[bk], buf("gv"), buf(f"sgc{i}")], writes=[zbufs[m]])
    SP_OA = [(0, 512, 6)]
    SP_OB = [(512, 512, 7)]
    out_toks = []
    fin = {"ld": 0}

    def c4_chunk(wi, wbf, ml, m, spieces, stream):
        res = mm_chunk(wi, wbf, ml * 128, zT, zbufs, [(c0, n) for (c0, n, _) in spieces])
        for (bk, c0, n) in res:
            P.do("act", lambda e, bk=bk, c0=c0, n=n, m=m: e.activation(out=ybuf[:, m, c0:c0 + n],
                                                                         in_=pb[bk][:, :n], func=AF.Copy),
                 reads=[bank[bk]], writes=[ybufs[m]])
        stats_add(stream, m, ybuf[:, m, :], ybufs[m], spieces, NKC)

    def ld_h1o(m, lo, hi):
        i = fin["ld"] % NS
        fin["ld"] += 1
        P.dma("sp", stg[i][:, lo:hi], h1v[:, m, 16 + lo:16 + hi], f"stg{i}", reads=[h1d[m]], writes=[buf(f"stg{i}")],
              extra=b_toks + spill_toks)
        return i

    def fin_chunk(i, m, lo, hi, k_add):
        sb = buf(f"stg{i}")
        hb = buf(f"h1c{i}")
        P.do("dve", lambda e: e.scalar_tensor_tensor(out=h1c[i][:, lo:hi], in0=ybuf[:, m, lo:hi],
                                                     scalar=gv[:, 3, m:m + 1], in1=rstd[:, lo:hi],
                                                     op0=ALU.mult, op1=ALU.mult),
             reads=[ybufs[m], buf("gv"), buf("rstd")], writes=[hb], extra=b_toks)
        P.do("pool" if k_add % 3 != 2 else "dve", lambda e: e.tensor_tensor(
            out=h1c[i][:, lo:hi], in0=h1c[i][:, lo:hi], in1=stg[i][:, lo:hi], op=ALU.add),
            reads=[sb, hb], writes=[hb])
        out_toks.append(P.dma("sp", outv[:, m, lo:hi], h1c[i][:, lo:hi], f"ost{i}", reads=[hb]))

    for cb_ in range(8):
        wi, wbf = get_w()
        for ml in range(2):
            c4_chunk(wi, wbf, ml, cb_ * 2 + ml, SP_OA, 0)
    stats_finish(SP_OA)
    pre = [ld_h1o(m, 0, 512) for m in range(NS - 1)]
    for cb_ in range(8):
        wi, wbf = get_w()
        for ml in range(2):
            m = cb_ * 2 + ml
            c4_chunk(wi, wbf, ml, m, SP_OB, 1)
            if m + NS - 1 < NKC:
                pre.append(ld_h1o(m + NS - 1, 0, 512))
            fin_chunk(pre.pop(0), m, 0, 512, m)
    stats_finish(SP_OB)
    pre = [ld_h1o(m, 512, 1024) for m in range(NS - 1)]
    for m in range(NKC):
        if m + NS - 1 < NKC:
            pre.append(ld_h1o(m + NS - 1, 512, 1024))
        fin_chunk(pre.pop(0), m, 512, 1024, m)
    return out_toks


ARENA_BYTES = 206 * 1024


def build(mode):
    nc = bass.Bass("TRN2", target_bir_lowering=False)
    P = Prog(nc)
    ar = Arena(nc, ARENA_BYTES, "arena")
    pb = [nc.alloc_psum_tensor(f"pb{i}", [128, 512], F32) for i in range(8)]

    def din(name, shape, dt=F32):
        return nc.dram_tensor(name, shape, dt, kind="ExternalInput").ap()

    toks = []
    if mode in ("A", "F"):
        ioA = {
            "hT": din("hT", [D, L]), "wA": din("wA", [D, 2048]), "gpre0": din("gpre0", [128, NKC]),
            "cosT": din("cosT", [32, L]), "sinT": din("sinT", [32, L]), "lamv": din("lamv", [128, 4]),
            "sublnG": din("sublnG", [128, 256]), "ident": din("ident", [128, 128]), "perm": din("perm", [32, 32]),
        }
    if mode in ("C", "F"):
        ioC = {
            "hTs": din("hTs", [D, TC]), "wo0": din("wo0", [D, D]), "wi1": din("wi1", [D, 2 * D]),
            "wg": din("wg", [4, 512, 512]), "wo1": din("wo1", [D, D]),
            "gpost0": din("gpost0", [128, NKC]), "gpre1": din("gpre1", [128, NKC]),
            "pscale": din("pscale", [128, NKC]), "gpost1": din("gpost1", [128, NKC]),
        }
        outT = nc.dram_tensor("outT", [D, 1024], F32, kind="ExternalOutput").ap()
        h1_scr = nc.dram_tensor("h1scr", [D, TC], F32).ap()
    if mode == "A":
        ogT_d = nc.dram_tensor("ogT", [9, 512, 512], BF16, kind="ExternalOutput").ap()
        toks = build_A(nc, P, ar, pb, ioA, ogT_d)
    elif mode == "C":
        ogT_in = din("ogTs", [D, TC], BF16)
        toks = build_C(nc, P, ar, pb, ioC, ogT_in, h1_scr, outT, [])
    else:
        cins = [nc.dram_tensor(f"cc_in{t}", [512, 512], BF16) for t in range(9)]
        cin_all = None
        couts = nc.dram_tensor("cc_out", [9, D, 512], BF16)
        ccs = nc.alloc_semaphore(name="ccsem")
        ncc = [0]

        def after_store(t, tok):
            P.wait("pool", [tok])

            def cc(e, t=t):
                e.collective_compute("AllGather", ALU.bypass, replica_groups=[[0, 1, 2, 3], [4, 5, 6, 7]],
                                     ins=[cins[t].ap().opt()], outs=[couts.ap()[t].opt()]).then_inc(ccs)
            P.raw("pool", cc)
            ncc[0] += 1

        ws = WStream(P, ar, ioC)
        ogT_view = ar.alloc([NKC, TC], BF16)
        pre_bytes = ar.off
        ar.reset(0)
        og_pre = {"bufs": [Buf(f"ogT{q}") for q in range(4)], "parts": (0, 1)}
        st = {}

        def ld(e):
            st["r2"] = (e.partition_id() % 4) * 2
        P.raw("sp", ld)
        coutv = couts.ap().rearrange("s (c p) t -> p c s t", p=128)

        def og_src(q4, part):
            if part == 0:
                return coutv[:, 4 * q4:4 * q4 + 4, bass.ds(st["r2"], 1), 496:512]
            return coutv[:, 4 * q4:4 * q4 + 4, bass.ds(st["r2"] + part, 1), :]

        def hook(wa_readers, dead_bytes):
            assert pre_bytes <= dead_bytes, (pre_bytes, dead_bytes)
            ws.issue_upto(ws.PREF, wa_readers)
            tc8 = (ccs, ncc[0])
            for q4 in range(4):
                for part in (0, 1):
                    c_lo, c_hi = (0, 16) if part == 0 else (16, 528)
                    P.dma("sp", ogT_view[:, 4 * q4:4 * q4 + 4, c_lo:c_hi], (lambda q4=q4, part=part: og_src(q4, part)),
                          f"ogin{part}_{q4}", writes=[og_pre["bufs"][q4]], extra=list(wa_readers) + [tc8])

        tA = build_A(nc, P, ar, pb, ioA, cins, after_store, after_last_proj=hook)
        tcc = (ccs, ncc[0])
        for e_ in Prog.ENG:
            P.wait(e_, tA)
        ar.reset(0)
        toks = build_C(nc, P, ar, pb, ioC, og_src, h1_scr, outT, [tcc], ws=ws, og_pre=og_pre)
    P.wait("sp", toks)
    P.emit()
    return nc


def _rope_tables():
    pos = np.arange(L, dtype=np.float32)
    inv = (np.float32(500000.0) ** (-np.arange(0, 32, 2, dtype=np.float32) / np.float32(32))).astype(np.float32)
    ang = (pos[:, None] * inv[None, :]).astype(np.float32)
    c = np.cos(ang).astype(np.float32).T
    s = np.sin(ang).astype(np.float32).T
    cosT = np.concatenate([c, c], 0)
    sinT = np.concatenate([-s, s], 0)
    return np.ascontiguousarray(cosT), np.ascontiguousarray(sinT)


def _pc(v):
    return np.ascontiguousarray(v.reshape(NKC, 128).T)


_CACHE = {}


def _get(mode):
    if mode not in _CACHE:
        _CACHE[mode] = build(mode)
    return _CACHE[mode]


def kernel(x, meta_tokens, pre_norm_g, post_norm_g, attn_w_in, attn_w_out,
           attn_lambda_q1, attn_lambda_k1, attn_lambda_q2, attn_lambda_k2, attn_subln_g,
           pool_w_in, pool_w_group, pool_scale, pool_w_out):
    f = np.float32
    x = np.asarray(x, f)
    B = x.shape[0]
    hT = [np.ascontiguousarray(np.concatenate([np.asarray(meta_tokens, f), x[b]], 0).T) for b in range(B)]
    cosT, sinT = _rope_tables()
    w_in = np.asarray(attn_w_in, f)[0]
    ident = np.eye(128, dtype=f)
    perm = np.zeros((32, 32), f)
    for i in range(32):
        perm[(i + 16) % 32, i] = 1.0
    lamv = np.ascontiguousarray(np.stack([np.asarray(a, f)[0] for a in
                                          (attn_lambda_q1, attn_lambda_k1, attn_lambda_q2, attn_lambda_k2)], 1))
    sublnG = np.ascontiguousarray(np.broadcast_to(np.asarray(attn_subln_g, f)[0][None, :], (128, 256)))
    gpre0 = _pc(np.asarray(pre_norm_g, f)[0])
    cm = {"wo0": np.asarray(attn_w_out, f)[0], "wi1": np.asarray(pool_w_in, f)[0],
          "wg": np.asarray(pool_w_group, f)[0], "wo1": np.asarray(pool_w_out, f)[0],
          "gpost0": _pc(np.asarray(post_norm_g, f)[0]), "gpre1": _pc(np.asarray(pre_norm_g, f)[1]),
          "pscale": _pc(np.asarray(pool_scale, f)[0]), "gpost1": _pc(np.asarray(post_norm_g, f)[1])}
    mapsA = []
    for core in range(8):
        b, r = core // 4, core % 4
        cols = np.concatenate([np.arange(q * 2048 + r * 512, q * 2048 + r * 512 + 512) for q in range(4)])
        mapsA.append({"hT": hT[b], "wA": np.ascontiguousarray(w_in[:, cols]), "gpre0": gpre0, "cosT": cosT,
                      "sinT": sinT, "lamv": lamv, "sublnG": sublnG, "ident": ident, "perm": perm})
    out = np.empty((B, SEQ, D), f)
    if FUSED:
        ncF = _get("F")
        in_maps = []
        for core in range(8):
            b, r = core // 4, core % 4
            d = dict(cm)
            d.update(mapsA[core])
            d["hTs"] = np.ascontiguousarray(hT[b][:, 1024 * r:1024 * r + TC])
            in_maps.append(d)
        res = run_bass_kernel_spmd(ncF, in_maps, core_ids=list(range(8)))
        for core in range(8):
            b, r = core // 4, core % 4
            out[b, 1024 * r:1024 * r + 1024, :] = np.asarray(res.results[core]["outT"]).T
        return out
    resA = run_bass_kernel_spmd(_get("A"), mapsA, core_ids=list(range(8)))
    def _asm(a):
        a = np.asarray(a)
        return np.concatenate([a[0][:, 496:512]] + [a[t_] for t_ in range(1, 9)], 1)
    ogT = [np.concatenate([_asm(resA.results[4 * b + r]["ogT"]) for r in range(4)], 0) for b in range(B)]
    in_maps = []
    for core in range(8):
        b, r = core // 4, core % 4
        d = dict(cm)
        d["hTs"] = np.ascontiguousarray(hT[b][:, 1024 * r:1024 * r + TC])
        d["ogTs"] = np.ascontiguousarray(ogT[b][:, 1024 * r:1024 * r + TC])
        in_maps.append(d)
    resC = run_bass_kernel_spmd(_get("C"), in_maps, core_ids=list(range(8)))
    for core in range(8):
        b, r = core // 4, core % 4
        out[b, 1024 * r:1024 * r + 1024, :] = np.asarray(resC.results[core]["outT"]).T
    return out
```

```python
import math
import numpy as np
import concourse.bass as bass
import concourse.mybir as mybir
from concourse.bass_utils import run_bass_kernel_spmd

F32 = mybir.dt.float32
BF16 = mybir.dt.bfloat16
AF = mybir.ActivationFunctionType
ALU = mybir.AluOpType
AX = mybir.AxisListType

D = 2048
SEQ = 4096
NMETA = 16
L = SEQ + NMETA
EPS = 1e-6
NKC = 16
TC = 1040
LAM_INIT0 = 0.8 - 0.6 * math.exp(-0.3 * 0)
SCALE = 128 ** -0.5
FUSED = True
NT_A = 9


class Buf:
    __slots__ = ("name", "w", "r")

    def __init__(self, name):
        self.name = name
        self.w = None
        self.r = []


class Prog:
    ENG = ("pe", "act", "dve", "pool", "sp")

    def __init__(self, nc):
        self.nc = nc
        self.ops = {e: [] for e in self.ENG}
        self.sem = {e: nc.alloc_semaphore(name=f"s_{e}") for e in self.ENG}
        self.cnt = {e: 0 for e in self.ENG}
        self.waited = {e: {} for e in self.ENG}
        self.dma_sems = {}
        self.pend_r = {e: [] for e in self.ENG}
        self.pend_w = {e: [] for e in self.ENG}

    def _waits(self, eng, deps):
        waits = []
        for d in deps:
            if d is None:
                continue
            if d == "pending":
                assert eng == "pe", "dependency on unresolved pending write"
                continue
            s, v = d
            if eng == "pe" and s is self.sem["pe"]:
                continue
            if self.waited[eng].get(s.num, 0) >= v:
                continue
            self.waited[eng][s.num] = v
            waits.append((s, v))
        return waits

    def _deps(self, reads, writes, extra):
        deps = list(extra)
        for b in reads:
            deps.append(b.w)
        for b in writes:
            deps.extend(b.r)
            deps.append(b.w)
        return deps

    def _commit(self, eng, tok, reads, writes):
        for b in self.pend_r[eng]:
            b.r.append(tok)
        for b in self.pend_w[eng]:
            b.w = tok
        self.pend_r[eng] = []
        self.pend_w[eng] = []
        for b in reads:
            b.r.append(tok)
        for b in writes:
            b.w = tok
            b.r = []

    def do(self, eng, fn, reads=(), writes=(), sig=True, extra=()):
        deps = self._deps(reads, writes, extra)
        waits = self._waits(eng, deps)
        tok = None
        if sig:
            self.cnt[eng] += 1
            tok = (self.sem[eng], self.cnt[eng])
        s_own = self.sem[eng]

        def run(e, fn=fn, waits=waits, sig=sig):
            for (s, v) in waits:
                e.wait_ge(s, v)
            ins = fn(e)
            if sig:
                ins.then_inc(s_own, 1)
        self.ops[eng].append(run)
        if sig:
            self._commit(eng, tok, reads, writes)
        else:
            self.pend_r[eng].extend(reads)
            for b in writes:
                b.r = []
                b.w = "pending"
                self.pend_w[eng].append(b)
        return tok

    def dma(self, eng, out, in_, sem_name, reads=(), writes=(), extra=()):
        if sem_name not in self.dma_sems:
            self.dma_sems[sem_name] = [self.nc.alloc_semaphore(name=f"d_{sem_name}"), 0]
        ent = self.dma_sems[sem_name]
        ent[1] += 16
        s, v = ent[0], ent[1]
        deps = self._deps(reads, writes, extra)
        waits = self._waits(eng, deps)

        def run(e, waits=waits):
            for (ws, wv) in waits:
                e.wait_ge(ws, wv)
            src = in_() if callable(in_) else in_
            e.dma_start(out=out, in_=src).then_inc(s, 16)
        self.ops[eng].append(run)
        tok = (s, v)
        for b in reads:
            b.r.append(tok)
        for b in writes:
            b.w = tok
            b.r = []
        return tok

    def wait(self, eng, deps):
        waits = self._waits(eng, deps)

        def run(e, waits=waits):
            for (ws, wv) in waits:
                e.wait_ge(ws, wv)
        self.ops[eng].append(run)

    def raw(self, eng, fn):
        self.ops[eng].append(fn)

    def emit(self):
        nc = self.nc
        with nc.Block() as block:
            @block.tensor
            def _(e):
                for f in self.ops["pe"]:
                    f(e)

            @block.scalar
            def _(e):
                for f in self.ops["act"]:
                    f(e)

            @block.vector
            def _(e):
                for f in self.ops["dve"]:
                    f(e)

            @block.gpsimd
            def _(e):
                for f in self.ops["pool"]:
                    f(e)

            @block.sync
            def _(e):
                for f in self.ops["sp"]:
                    f(e)


class Arena:
    def __init__(self, nc, nbytes, name):
        self.t = nc.alloc_sbuf_tensor(name, [128, nbytes // 4], F32)
        self.cap = nbytes
        self.off = 0

    def reset(self, off=0):
        self.off = off

    def alloc(self, shape, dtype):
        n = 1
        for s in shape:
            n *= s
        sz = 2 if dtype == BF16 else 4
        nb = (n * sz + 31) // 32 * 32
        assert self.off + nb <= self.cap, f"arena overflow {self.off}+{nb}>{self.cap}"
        v = self.t[:, self.off // 4:(self.off + nb) // 4]
        if dtype == BF16:
            v = v.bitcast(BF16)
        v = v[:, 0:n]
        self.off += nb
        if len(shape) == 2:
            return v.rearrange("p (a b) -> p a b", a=shape[0])
        if len(shape) == 3:
            return v.rearrange("p (a b c) -> p a b c", a=shape[0], b=shape[1])
        return v


def build_A(nc, P, ar, pb, io, og_slots, after_store=None, after_last_proj=None):
    hTv = io["hT"].rearrange("(c p) t -> p c t", p=128)
    wAv = io["wA"].rearrange("(c p) n -> p c n", p=128)
    if isinstance(og_slots, list):
        ogv_t = [h.ap().rearrange("(c p) t -> p c t", p=128) for h in og_slots]
    else:
        ogv_t = [og_slots[t_].rearrange("(c p) t -> p c t", p=128) for t_ in range(9)]

    wA = ar.alloc([NKC, 2048], BF16)
    xraw = ar.alloc([4096], F32)
    xb = xraw.bitcast(BF16).rearrange("p (c t) -> p c t", c=NKC)
    sq = ar.alloc([NKC, 512], BF16)
    dead_mark = ar.off
    kT = ar.alloc([4, L], BF16)
    Vp = ar.alloc([33, 2, 257], BF16)
    qT = ar.alloc([4, 512], BF16)
    sg = ar.alloc([4, 512], F32)
    rstd_fm = ar.alloc([512], F32)
    rstd_tm = ar.alloc([4], F32)
    cosb = ar.alloc([512], F32)
    sinb = ar.alloc([512], F32)
    t1 = ar.alloc([512], F32)
    t2 = ar.alloc([512], F32)
    NE = 5
    e_sb = [ar.alloc([512], BF16) for _ in range(NE)]
    o_sb = ar.alloc([4, 256], F32)
    og_sb = ar.alloc([4, 512], BF16)
    ogT_st = ar.alloc([4, 512], BF16)
    ident = ar.alloc([128], BF16)
    perm = ar.alloc([32], BF16)
    onesM = ar.alloc([128], BF16)
    onesF = ar.alloc([128], F32)
    gpre = ar.alloc([NKC], F32)
    lamv = ar.alloc([4], F32)
    lamt = ar.alloc([8], F32)
    sublnG = ar.alloc([256], F32)
    small = ar.alloc([64], F32)
    epsc = ar.alloc([8], F32)
    tmp256 = ar.alloc([256], F32)
    tmp256b = ar.alloc([256], F32)
    wstg = [xraw[:, 0:2048], xraw[:, 2048:4096]]

    B = {}

    def buf(name):
        if name not in B:
            B[name] = Buf(name)
        return B[name]

    bank = [buf(f"bank{i}") for i in range(8)]

    P.dma("sp", gpre, io["gpre0"], "c0", writes=[buf("gpre")])
    P.dma("sp", lamv, io["lamv"], "c1", writes=[buf("lamv")])
    P.dma("sp", sublnG, io["sublnG"], "c2", writes=[buf("sublnG")])
    P.dma("pool", ident, io["ident"], "c3", writes=[buf("ident")])
    P.dma("pool", perm[0:32, :], io["perm"], "c4", writes=[buf("perm")])
    P.do("dve", lambda e: e.memset(onesM, 2.0 ** -11), writes=[buf("onesM")])
    P.do("dve", lambda e: e.memset(onesF, 1.0), writes=[buf("onesF")])
    P.do("pool", lambda e: e.memset(Vp[:, :, :, 256:257], 1.0), writes=[buf("Vp")])
    P.do("dve", lambda e: e.memset(epsc[:, 0:1], EPS), writes=[buf("epsc")])
    P.do("dve", lambda e: e.memset(epsc[:, 1:2], math.log(1.0 - LAM_INIT0)), writes=[buf("epsc")])

    P.do("dve", lambda e: e.tensor_tensor(out=lamt[:, 0:1], in0=lamv[:, 0:1], in1=lamv[:, 1:2], op=ALU.mult),
         reads=[buf("lamv")], writes=[buf("lamt")])
    P.do("dve", lambda e: e.tensor_tensor(out=lamt[:, 1:2], in0=lamv[:, 2:3], in1=lamv[:, 3:4], op=ALU.mult),
         reads=[buf("lamv")], writes=[buf("lamt")])
    P.do("pe", lambda e: e.matmul(pb[7][:, 0:2], lhsT=onesF, rhs=lamt[:, 0:2], start=True, stop=True),
         reads=[buf("onesF"), buf("lamt")], writes=[bank[7]])
    P.do("act", lambda e: e.activation(out=lamt[:, 2:4], in_=pb[7][:, 0:2], func=AF.Exp),
         reads=[bank[7]], writes=[buf("lamt2")])
    P.do("dve", lambda e: e.scalar_tensor_tensor(out=lamt[:, 4:5], in0=lamt[:, 3:4], scalar=-LAM_INIT0,
                                                 in1=lamt[:, 2:3], op0=ALU.add, op1=ALU.subtract),
         reads=[buf("lamt2")], writes=[buf("neglam")])
    neglam = lamt[:, 4:5]

    WT = []
    for kc in range(NKC):
        sb = buf(f"wstg{kc % 2}")
        P.dma("sp", wstg[kc % 2], wAv[:, kc, :], f"wstg{kc % 2}", writes=[sb])
        wt = P.do("dve", lambda e, kc=kc: e.tensor_scalar(out=wA[:, kc, :], in0=wstg[kc % 2],
                                                          scalar1=gpre[:, kc:kc + 1], scalar2=None, op0=ALU.mult),
                  reads=[sb, buf("gpre")], writes=[buf("wA")])
        WT.append(wt)

    tiles = [(0, 16)] + [(16 + 512 * i, 512) for i in range(8)]

    def tile_blocks(t):
        if t == 0:
            return [(0, 0, 16)]
        return [(4 * (t - 1) + 1 + j, 128 * j, 128) for j in range(4)]

    wtoks = WT[-2:]

    def prep_load(t):
        pos0, NT = tiles[t]
        for q4 in range(4):
            P.dma("pool", xb[:, 4 * q4:4 * q4 + 4, :NT], hTv[:, 4 * q4:4 * q4 + 4, pos0:pos0 + NT], f"xb{q4}",
                  writes=[buf(f"xb{q4}")], extra=(wtoks if t == 0 else ()))
        P.dma("sp", cosb[0:32, :NT], io["cosT"][:, pos0:pos0 + NT], "cos", writes=[buf("cos")])
        P.dma("sp", sinb[0:32, :NT], io["sinT"][:, pos0:pos0 + NT], "sin", writes=[buf("sin")])

    def prep_squares(t):
        pos0, NT = tiles[t]
        for q4 in range(4):
            P.do("pool", lambda e, q4=q4: e.tensor_tensor(out=sq[:, 4 * q4:4 * q4 + 4, :NT],
                                                          in0=xb[:, 4 * q4:4 * q4 + 4, :NT],
                                                          in1=xb[:, 4 * q4:4 * q4 + 4, :NT], op=ALU.mult),
                 reads=[buf(f"xb{q4}")], writes=[buf(f"sq{q4}")])

    def prep_stats(t):
        pos0, NT = tiles[t]
        blks = tile_blocks(t)
        for q4 in range(4):
            sqb = buf(f"sq{q4}")
            for k4 in range(4):
                kc = 4 * q4 + k4
                P.do("pe", lambda e, kc=kc: e.matmul(pb[6][:, :NT], lhsT=onesM, rhs=sq[:, kc, :NT],
                                                     start=(kc == 0), stop=(kc == NKC - 1)),
                     reads=[sqb, buf("onesM")], writes=[bank[6]], sig=False)
                for bi, (kb, off, ntok) in enumerate(blks):
                    last = (bi == len(blks) - 1 and k4 == 3)
                    P.do("pe", lambda e, kc=kc, bi=bi, off=off, ntok=ntok: e.matmul(
                        pb[7][:ntok, bi:bi + 1], lhsT=sq[:, kc, off:off + ntok], rhs=onesM[:, 0:1],
                        start=(kc == 0 and bi == 0), stop=(kc == NKC - 1), skip_group_check=True),
                        reads=[sqb, buf("onesM")], writes=[bank[7]], sig=last)
        P.do("act", lambda e: e.activation(out=rstd_fm[:, :NT], in_=pb[6][:, :NT], func=AF.Ln, bias=epsc[:, 0:1]),
             reads=[bank[6], buf("epsc")], writes=[buf("rstd_fm")])
        P.do("act", lambda e: e.activation(out=rstd_fm[:, :NT], in_=rstd_fm[:, :NT], func=AF.Exp, scale=-0.5),
             reads=[buf("rstd_fm")], writes=[buf("rstd_fm")])
        nb = len(blks)
        np_ = blks[0][2]
        P.do("act", lambda e: e.activation(out=rstd_tm[:np_, :nb], in_=pb[7][:np_, :nb], func=AF.Ln,
                                           bias=epsc[:np_, 0:1]),
             reads=[bank[7], buf("epsc")], writes=[buf("rstd_tm")])
        P.do("act", lambda e: e.activation(out=rstd_tm[:np_, :nb], in_=rstd_tm[:np_, :nb], func=AF.Exp, scale=-0.5),
             reads=[buf("rstd_tm")], writes=[buf("rstd_tm")])

    pj = [0]

    def proj(t):
        pos0, NT = tiles[t]
        pj[0] = 0
        xbufs = [buf(f"xb{q}") for q in range(4)]
        ropes = []

        def do_rope(dst, dbuf):
            d32 = dst[0:32, :]
            P.do("pe", lambda e, d32=d32: e.matmul(pb[7][0:32, :NT], lhsT=perm[0:32, 0:32], rhs=d32,
                                                   start=True, stop=True),
                 reads=[dbuf, buf("perm")], writes=[bank[7]])
            P.do("pool", lambda e, d32=d32: e.tensor_tensor(out=t1[0:32, :NT], in0=d32, in1=cosb[0:32, :NT],
                                                            op=ALU.mult),
                 reads=[dbuf, buf("cos")], writes=[buf("t1")])
            P.do("dve", lambda e: e.tensor_tensor(out=t2[0:32, :NT], in0=pb[7][0:32, :NT], in1=sinb[0:32, :NT],
                                                  op=ALU.mult),
                 reads=[bank[7], buf("sin")], writes=[buf("t2")])
            P.do("pool", lambda e, d32=d32: e.tensor_tensor(out=d32, in0=t1[0:32, :NT], in1=t2[0:32, :NT],
                                                            op=ALU.add),
                 reads=[buf("t1"), buf("t2")], writes=[dbuf])

        for oc in range(8):
            bk = pj[0] % 4
            pj[0] += 1
            for kc in range(NKC):
                P.do("pe", lambda e, kc=kc, oc=oc, bk=bk: e.matmul(
                    pb[bk][:, :NT], lhsT=wA[:, kc, oc * 128:(oc + 1) * 128], rhs=xb[:, kc, :NT],
                    start=(kc == 0), stop=(kc == NKC - 1)),
                    reads=[buf("wA"), xbufs[kc // 4]], writes=[bank[bk]], sig=(kc == NKC - 1))
            if oc < 4:
                dst = qT[:, oc, :NT]
                dbuf = buf(f"qT{oc}")
            else:
                dst = kT[:, oc - 4, pos0:pos0 + NT]
                dbuf = buf(f"kT{oc - 4}")
            P.do("dve", lambda e, dst=dst, bk=bk: e.tensor_tensor(out=dst, in0=pb[bk][:, :NT], in1=rstd_fm[:, :NT],
                                                                  op=ALU.mult),
                 reads=[bank[bk], buf("rstd_fm")], writes=[dbuf])
            ropes.append((dst, dbuf))
            if len(ropes) > 1:
                do_rope(*ropes.pop(0))
            if oc == 0:
                advance_subln(1)
            if oc == 2:
                advance_subln(2)
            if oc == 6:
                flush_deferred()
        blks = tile_blocks(t)
        for bi, (kb, off, ntok) in enumerate(blks):
            for which in range(2):
                bk = pj[0] % 4
                pj[0] += 1
                c0 = 1024 + 512 * which
                for kc in range(NKC):
                    P.do("pe", lambda e, kc=kc, bk=bk, off=off, ntok=ntok, c0=c0: e.matmul(
                        pb[bk][:ntok, 0:512], lhsT=xb[:, kc, off:off + ntok], rhs=wA[:, kc, c0:c0 + 512],
                        start=(kc == 0), stop=(kc == NKC - 1)),
                        reads=[buf("wA"), xbufs[kc // 4]], writes=[bank[bk]], sig=(kc == NKC - 1))
                if which == 0:
                    P.do("act", lambda e, bk=bk, kb=kb, bi=bi, ntok=ntok: e.activation(
                        out=Vp[:ntok, kb, :, 0:256], in_=pb[bk][:ntok, 0:512].rearrange("p (h d) -> p h d", h=2),
                        func=AF.Copy, scale=rstd_tm[:ntok, bi:bi + 1]),
                        reads=[bank[bk], buf("rstd_tm")], writes=[buf("Vp")])
                else:
                    P.do("act", lambda e, bk=bk, bi=bi, ntok=ntok: e.activation(
                        out=sg[:ntok, bi, :], in_=pb[bk][:ntok, 0:512], func=AF.Silu,
                        scale=rstd_tm[:ntok, bi:bi + 1]),
                        reads=[bank[bk], buf("rstd_tm")], writes=[buf("sg")])
            if bi == 0:
                while ropes:
                    do_rope(*ropes.pop(0))

    deferred = []
    pending_subln = []

    def advance_subln(upto):
        for d in pending_subln:
            if d["done"] < 1 and upto >= 1:
                d["A"]()
                d["done"] = 1
            if d["done"] < 2 and upto >= 2:
                d["B"]()
                d["done"] = 2
        pending_subln[:] = [d for d in pending_subln if d["done"] < 2]

    def flush_deferred():
        advance_subln(2)
        while deferred:
            deferred.pop(0)()

    sc = [0]
    ec = [0]

    def attn_head(t, h):
        pos0, NT = tiles[t]
        blks = tile_blocks(t)
        nqb = len(blks)
        nq = blks[0][2]
        keys = [(0, 0, 16, 0)]
        if t > 0:
            for kb in range(1, 4 * (t - 1) + 1):
                keys.append((kb, 16 + 128 * (kb - 1), 128, 0))
            for j in range(4):
                kb = 4 * (t - 1) + 1 + j
                keys.append((kb, 16 + 128 * (kb - 1), 128, j))
        LOOK = 3
        sc[0] = 0
        items = [(m, ki) + keys[ki] for m in range(2) for ki in range(len(keys))]

        def emit_S(it):
            m, ki, kb, kpos, nk, jv = it
            qi = 2 * h + m
            c0 = jv * 128
            ncols = NT - c0
            sbk = 4 + (sc[0] % 4)
            sc[0] += 1
            ei = ec[0] % NE
            ec[0] += 1
            eb = buf(f"e{ei}")
            P.do("pe", lambda e, sbk=sbk, qi=qi, kpos=kpos, nk=nk, c0=c0, ncols=ncols: e.matmul(
                pb[sbk][:nk, :ncols], lhsT=kT[:, qi, kpos:kpos + nk], rhs=qT[:, qi, c0:c0 + ncols],
                start=True, stop=True),
                reads=[buf(f"kT{qi}"), buf(f"qT{qi}")], writes=[bank[sbk]])
            P.do("act", lambda e, sbk=sbk, ei=ei, nk=nk, ncols=ncols: e.activation(
                out=e_sb[ei][:nk, :ncols], in_=pb[sbk][:nk, :ncols], func=AF.Exp, scale=SCALE),
                reads=[bank[sbk]], writes=[eb])
            if t > 0 and kb >= 4 * (t - 1) + 1:
                P.do("dve", lambda e, ei=ei: e.memset(e_sb[ei][64:128, 0:64], 0.0), writes=[eb])
            return ei

        def emit_PV(it, ei):
            m, ki, kb, kpos, nk, jv = it
            eb = buf(f"e{ei}")
            for j in range(jv, nqb):
                lc = (j - jv) * 128 if t > 0 else 0
                first = (ki == 0)
                last = (kb == blks[j][0])
                lastpv = (j == nqb - 1)
                P.do("pe", lambda e, ei=ei, nk=nk, lc=lc, j=j, kb=kb, first=first, last=last: e.matmul(
                    pb[j][:nq, 0:257], lhsT=e_sb[ei][:nk, lc:lc + nq], rhs=Vp[:nk, kb, h, :],
                    start=first, stop=last),
                    reads=[eb, buf("Vp")], writes=[bank[j]], sig=(last or lastpv))
                if last:
                    evac(m, j)

        def evac(m, j):
            if j == 0:
                advance_subln(2)
            if True:
                sm = small[:nq, 4 * j:4 * j + 4]
                sbf = buf(f"small{j}")
                P.do("dve", lambda e, j=j, sm=sm: e.reciprocal(out=sm[:, 0:1], in_=pb[j][:nq, 256:257]),
                     reads=[bank[j]], writes=[sbf])
                if m == 0:
                    P.do("dve", lambda e, j=j, sm=sm: e.tensor_scalar(out=o_sb[:nq, j, :], in0=pb[j][:nq, 0:256],
                                                                      scalar1=sm[:, 0:1], scalar2=None, op0=ALU.mult),
                         reads=[bank[j], sbf], writes=[buf(f"o{j}")])
                else:
                    P.do("dve", lambda e, sm=sm: e.tensor_tensor(out=sm[:, 1:2], in0=sm[:, 0:1], in1=neglam[:nq, :],
                                                                 op=ALU.mult),
                         reads=[sbf, buf("neglam")], writes=[sbf])
                    P.do("dve", lambda e, j=j, sm=sm: e.scalar_tensor_tensor(
                        out=o_sb[:nq, j, :], in0=pb[j][:nq, 0:256], scalar=sm[:, 1:2], in1=o_sb[:nq, j, :],
                        op0=ALU.mult, op1=ALU.add),
                        reads=[bank[j], sbf, buf(f"o{j}")], writes=[buf(f"o{j}")])

        pend = []
        for n_it, it in enumerate(items):
            ei = emit_S(it)
            pend.append((it, ei))
            if len(pend) > LOOK:
                emit_PV(*pend.pop(0))
            if n_it == 3:
                advance_subln(1)
            if n_it == 9:
                advance_subln(2)
        while pend:
            emit_PV(*pend.pop(0))
        pending_subln.append({"A": lambda: subln_A(t, h), "B": lambda: subln_B(t, h), "done": 0})

    def subln_A(t, h):
        blks = tile_blocks(t)
        nq = blks[0][2]
        for j in range(len(blks)):
            sm = small[:nq, 16 + 4 * j:16 + 4 * j + 4]
            P.do("pool", lambda e, j=j: e.tensor_tensor(out=tmp256[:nq, :], in0=o_sb[:nq, j, :], in1=o_sb[:nq, j, :],
                                                        op=ALU.mult),
                 reads=[buf(f"o{j}")], writes=[buf("tmp256")])
            P.do("dve", lambda e, sm=sm: e.tensor_reduce(out=sm[:, 0:1], in_=tmp256[:nq, :], axis=AX.X, op=ALU.add),
                 reads=[buf("tmp256")], writes=[buf(f"small2{j}")])

    def subln_B(t, h):
        blks = tile_blocks(t)
        nq = blks[0][2]
        for j in range(len(blks)):
            sm = small[:nq, 16 + 4 * j:16 + 4 * j + 4]
            sbf = buf(f"small2{j}")
            P.do("act", lambda e, sm=sm: e.activation(out=sm[:, 1:2], in_=sm[:, 0:1], func=AF.Ln, scale=1.0 / 256,
                                                      bias=epsc[:nq, 0:1]),
                 reads=[sbf, buf("epsc")], writes=[sbf])
            P.do("act", lambda e, sm=sm: e.activation(out=sm[:, 2:3], in_=sm[:, 1:2], func=AF.Exp, scale=-0.5,
                                                      bias=epsc[:nq, 1:2]),
                 reads=[sbf, buf("epsc")], writes=[sbf])
        for j in range(len(blks)):
            sm = small[:nq, 16 + 4 * j:16 + 4 * j + 4]
            sbf = buf(f"small2{j}")
            P.do("dve", lambda e, j=j, sm=sm: e.scalar_tensor_tensor(
                out=tmp256b[:nq, :], in0=o_sb[:nq, j, :], scalar=sm[:, 2:3], in1=sublnG[:nq, :],
                op0=ALU.mult, op1=ALU.mult),
                reads=[buf(f"o{j}"), sbf, buf("sublnG")], writes=[buf("tmp256b")])
            P.do("pool", lambda e, j=j: e.tensor_tensor(out=og_sb[:nq, j, h * 256:(h + 1) * 256], in0=tmp256b[:nq, :],
                                                        in1=sg[:nq, j, h * 256:(h + 1) * 256], op=ALU.mult),
                 reads=[buf("tmp256b"), buf("sg")], writes=[buf(f"og{j}")])

    def finish_tile(t):
        pos0, NT = tiles[t]
        blks = tile_blocks(t)
        nq = blks[0][2]
        for j in range(len(blks)):
            tb = 4 + (j % 2)
            pbT = pb[tb][:, :].bitcast(BF16)
            for c in range(4):
                P.do("pe", lambda e, j=j, c=c, pbT=pbT: e.transpose(pbT[:, c * 128:c * 128 + nq],
                                                                    og_sb[:nq, j, c * 128:(c + 1) * 128],
                                                                    ident[:nq, :nq]),
                     reads=[buf(f"og{j}"), buf("ident")], writes=[bank[tb]], sig=(c == 3))
            P.do("dve", lambda e, j=j, pbT=pbT: e.tensor_copy(
                out=ogT_st[:, :, j * 128:j * 128 + nq],
                in_=pbT[:, 0:512].rearrange("p (c q) -> p c q", c=4)[:, :, 0:nq]),
                reads=[bank[tb]], writes=[buf("ogT_st")])
        dst = ogv_t[t][:, :, 512 - NT:512]
        tok = P.dma("sp", dst, ogT_st[:, :, :NT], "ogst", reads=[buf("ogT_st")])
        if after_store is not None:
            after_store(t, tok)
        return tok

    P.do("dve", lambda e: e.memset(ogT_st, 0.0), writes=[buf("ogT_st")])
    P.dma("sp", ogv_t[0][:, :, 0:496], ogT_st[:, :, 0:496], "ogz", reads=[buf("ogT_st")])
    out_toks = []
    prep_load(0)
    prep_squares(0)
    prep_stats(0)
    for t in range(NT_A):
        proj(t)
        if t + 1 < NT_A:
            prep_load(t + 1)
            prep_squares(t + 1)
        elif after_last_proj is not None:
            assert not P.pend_r["pe"]
            deadb = [buf("wA")] + [buf(f"xb{q}") for q in range(4)] + [buf(f"sq{q}") for q in range(4)]
            toks_ = []
            for b_ in deadb:
                toks_.extend(b_.r)
                toks_.append(b_.w)
            after_last_proj(toks_, dead_mark)
        attn_head(t, 0)
        if t + 1 < NT_A:
            prep_stats(t + 1)
        attn_head(t, 1)
        deferred.append(lambda t=t: out_toks.append(finish_tile(t)))
    flush_deferred()
    return out_toks


class WStream:
    NW = 4
    PREF = 3

    def __init__(self, P, ar, io):
        self.P = P
        self.wbuf = [ar.alloc([NKC, 256], BF16) for _ in range(self.NW)]
        wo0 = io["wo0"].rearrange("(c p) n -> p c n", p=128)
        wi1 = io["wi1"].rearrange("(c p) n -> p c n", p=128)
        wgv = io["wg"].rearrange("g (c p) n -> p g c n", p=128)
        wo1 = io["wo1"].rearrange("(c p) n -> p c n", p=128)
        wl = []
        for rep in range(2):
            for cb_ in range(8):
                wl.append(wo0[:, :, cb_ * 256:(cb_ + 1) * 256])
        for gi in range(4):
            for hb in range(2):
                wl.append(wi1[:, :, gi * 512 + hb * 256:gi * 512 + (hb + 1) * 256])
            for hb in range(2):
                wl.append(wi1[:, :, 2048 + gi * 512 + hb * 256:2048 + gi * 512 + (hb + 1) * 256])
                wl.append(wgv[:, gi, :, hb * 256:(hb + 1) * 256])
        for rep in range(2):
            for cb_ in range(8):
                wl.append(wo1[:, :, cb_ * 256:(cb_ + 1) * 256])
        self.wlist = wl
        self.bufs = [Buf(f"w{i}") for i in range(self.NW)]
        self.issued = 0
        self.next = 0

    def issue_upto(self, n, extra=()):
        while self.issued < min(len(self.wlist), n):
            idx = self.issued
            v = self.wlist[idx]
            i = idx % self.NW
            self.P.dma("pool", self.wbuf[i][:, :v.shape[1], :], v, f"w{i}", writes=[self.bufs[i]], extra=extra)
            self.issued += 1

    def get(self):
        idx = self.next
        self.next += 1
        self.issue_upto(idx + self.PREF)
        return idx % self.NW, self.bufs[idx % self.NW]


def build_C(nc, P, ar, pb, io, ogT_src, h1_scr, out_dst, deps_in, ws=None, og_pre=None):
    if callable(ogT_src):
        og_src = ogT_src
    else:
        ogv = ogT_src.rearrange("(c p) t -> p c t", p=128)

        def og_src(q4, part):
            if part == 0:
                return ogv[:, 4 * q4:4 * q4 + 4, 0:16]
            return ogv[:, 4 * q4:4 * q4 + 4, 16 + 512 * (part - 1):16 + 512 * part]
    h0v = io["hTs"].rearrange("(c p) t -> p c t", p=128)
    h1v = h1_scr.rearrange("(c p) t -> p c t", p=128)
    outv = out_dst.rearrange("(c p) t -> p c t", p=128)
    CP = [(0, 512), (512, 512), (1024, 16)]
    CPO = [(16, 512), (528, 512)]
    PO = [(0, 512), (512, 512)]

    B = {}

    def buf(name):
        if name not in B:
            B[name] = Buf(name)
        return B[name]

    bank = [buf(f"cbank{i}") for i in range(8)]
    if ws is None:
        ws = WStream(P, ar, io)
    else:
        ar.reset(ws.NW * NKC * 256 * 2)
    wbuf = ws.wbuf
    get_w = ws.get
    ogT = ar.alloc([NKC, TC], BF16)
    zT = ogT
    ymark = ar.off
    ybuf = ar.alloc([NKC, TC], F32)
    yend = ar.off
    h1n = ar.alloc([NKC, TC], BF16)
    NS = 3
    sqc = [ar.alloc([TC], BF16) for _ in range(4)]
    stg = [ar.alloc([TC], F32) for _ in range(NS)]
    h1c = [ar.alloc([TC], F32) for _ in range(NS)]
    rstd = ar.alloc([TC], F32)
    onesM = ar.alloc([128], BF16)
    gv = ar.alloc([4, NKC], F32)
    epsc = ar.alloc([8], F32)
    endmark = ar.off
    ar.reset(ymark)
    u_sb = [ar.alloc([TC], F32) for _ in range(2)]
    s2 = [ar.alloc([TC], F32) for _ in range(2)]
    s4 = [ar.alloc([TC], F32) for _ in range(2)]
    mixed = ar.alloc([4, 1024], BF16)
    sgc = [ar.alloc([1024], F32) for _ in range(2)]
    assert ar.off <= yend
    ar.reset(endmark)

    P.do("dve", lambda e: e.memset(onesM, 2.0 ** -11), writes=[buf("onesM")])
    P.do("dve", lambda e: e.memset(epsc[:, 0:1], EPS), writes=[buf("epsc")])
    for i, nm in enumerate(["gpost0", "gpre1", "pscale", "gpost1"]):
        P.dma("sp", gv[:, i, :], io[nm], f"cg{i}", writes=[buf("gv")])
    if og_pre is not None:
        for q4 in range(4):
            B[f"ogT{q4}"] = og_pre["bufs"][q4]
    for q4 in range(4):
        for part in (0, 1, 2):
            if og_pre is not None and part in og_pre["parts"]:
                continue
            c_lo, c_hi = (0, 16) if part == 0 else (16 + 512 * (part - 1), 16 + 512 * part)
            P.dma("sp", ogT[:, 4 * q4:4 * q4 + 4, c_lo:c_hi], (lambda q4=q4, part=part: og_src(q4, part)),
                  f"ogin{part}_{q4}", writes=[buf(f"ogTB{q4}" if part == 2 else f"ogT{q4}")], extra=deps_in)

    cb = [0]

    def next_bank():
        b = cb[0] % 5
        cb[0] += 1
        return b

    def mm_chunk(wi, wbf, wcol, act, act_bufs, pieces, nk=NKC):
        res = []
        for (c0, n) in pieces:
            bk = next_bank()
            for kc in range(nk):
                P.do("pe", lambda e, kc=kc, bk=bk, c0=c0, n=n: e.matmul(
                    pb[bk][:, :n], lhsT=wbuf[wi][:, kc, wcol:wcol + 128], rhs=act[:, kc, c0:c0 + n],
                    start=(kc == 0), stop=(kc == nk - 1)),
                    reads=[wbf] + act_bufs, writes=[bank[bk]], sig=(kc == nk - 1))
            res.append((bk, c0, n))
        return res

    def stats_add(stream, k, src, src_buf, pieces, total):
        i = 2 * stream + (k % 2)
        sb = buf(f"sqc{i}")
        lo = pieces[0][0]
        hi = pieces[-1][0] + pieces[-1][1]
        while len(stat_pend) > stat_depth[0]:
            stat_pend.pop(0)()
        P.do("act", lambda e, i=i: e.activation(out=sqc[i][:, lo:hi], in_=src[:, lo:hi], func=AF.Square),
             reads=[src_buf], writes=[sb])

        def pe_part():
            for pi, (c0, n, sbk) in enumerate(pieces):
                P.do("pe", lambda e, i=i, c0=c0, n=n, k=k, sbk=sbk: e.matmul(
                    pb[sbk][:, :n], lhsT=onesM, rhs=sqc[i][:, c0:c0 + n], start=(k == 0), stop=(k == total - 1),
                    skip_group_check=True),
                    reads=[sb, buf("onesM")], writes=[bank[sbk]], sig=(pi == len(pieces) - 1))
        stat_pend.append(pe_part)

    stat_pend = []
    stat_depth = [1]

    def stats_finish(pieces):
        while stat_pend:
            stat_pend.pop(0)()
        for (c0, n, sbk) in pieces:
            P.do("act", lambda e, sbk=sbk, c0=c0, n=n: e.activation(out=rstd[:, c0:c0 + n], in_=pb[sbk][:, :n],
                                                                    func=AF.Ln, bias=epsc[:, 0:1]),
                 reads=[bank[sbk], buf("epsc")], writes=[buf("rstd")])
            P.do("act", lambda e, c0=c0, n=n: e.activation(out=rstd[:, c0:c0 + n], in_=rstd[:, c0:c0 + n],
                                                           func=AF.Exp, scale=-0.5),
                 reads=[buf("rstd")], writes=[buf("rstd")])

    SP_A = [(0, 16, 5), (16, 512, 6)]
    SP_B = [(528, 512, 7)]
    SP_ALL = [(0, 512, 6), (512, 512, 7), (1024, 16, 5)]
    SP_O = [(0, 512, 6), (512, 512, 7)]
    ogbufs_s = [[buf(f"ogT{q}") for q in range(4)], [buf(f"ogTB{q}") for q in range(4)]]
    ybufs = [buf(f"y{m}") for m in range(NKC)]
    h1d = [buf(f"h1d{m}") for m in range(NKC)]

    def c1_chunk(wi, wbf, ml, m, spieces, stream):
        res = mm_chunk(wi, wbf, ml * 128, ogT, ogbufs_s[stream], [(c0, n) for (c0, n, _) in spieces])
        for (bk, c0, n) in res:
            P.do("act", lambda e, bk=bk, c0=c0, n=n, m=m: e.activation(out=ybuf[:, m, c0:c0 + n],
                                                                         in_=pb[bk][:, :n], func=AF.Copy),
                 reads=[bank[bk]], writes=[ybufs[m]])
        stats_add(stream, m, ybuf[:, m, :], ybufs[m], spieces, NKC)

    ldc = [0]
    spill_toks = []
    last_spill = {}
    a_toks = []

    def ld_h0(m, lo, hi):
        i = ldc[0] % NS
        ldc[0] += 1
        P.dma("sp", stg[i][:, lo:hi], h0v[:, m, lo:hi], f"stg{i}", writes=[buf(f"stg{i}")])
        return i

    def c2_chunk(i, m, lo, hi, spieces, stream, k_add):
        sb = buf(f"stg{i}")
        hb = buf(f"h1c{i}")
        P.do("dve", lambda e: e.scalar_tensor_tensor(out=h1c[i][:, lo:hi], in0=ybuf[:, m, lo:hi],
                                                     scalar=gv[:, 0, m:m + 1], in1=rstd[:, lo:hi],
                                                     op0=ALU.mult, op1=ALU.mult),
             reads=[ybufs[m], buf("gv"), buf("rstd")], writes=[hb])
        a_toks.append(P.do("pool" if k_add % 3 != 2 else "dve", lambda e: e.tensor_tensor(
            out=ybuf[:, m, lo:hi], in0=h1c[i][:, lo:hi], in1=stg[i][:, lo:hi], op=ALU.add),
            reads=[sb, hb], writes=[ybufs[m]]))
        spill_toks.append(P.dma("sp", h1v[:, m, lo:hi], ybuf[:, m, lo:hi], f"h1st{m % 4}", reads=[ybufs[m]],
                                writes=[h1d[m]], extra=last_spill.get(m % 4, [])))
        last_spill[m % 4] = [spill_toks[-1]]
        stats_add(stream, m, ybuf[:, m, :], ybufs[m], spieces, NKC)

    for cb_ in range(8):
        wi, wbf = get_w()
        for ml in range(2):
            c1_chunk(wi, wbf, ml, cb_ * 2 + ml, SP_A, 0)
    stats_finish(SP_A)
    def pass2(m, lo, hi):
        P.do("dve", lambda e: e.scalar_tensor_tensor(
            out=h1n[:, m, lo:hi], in0=ybuf[:, m, lo:hi], scalar=gv[:, 1, m:m + 1], in1=rstd[:, lo:hi],
            op0=ALU.mult, op1=ALU.mult),
            reads=[ybufs[m], buf("gv"), buf("rstd")], writes=[buf(f"h1n{m // 4}")])

    pre = [ld_h0(m, 0, 528) for m in range(NS - 1)]
    ca = 0
    cp2 = 0
    stat_depth[0] = 2
    for cb_ in range(8):
        wi, wbf = get_w()
        for ml in range(2):
            m = cb_ * 2 + ml
            c1_chunk(wi, wbf, ml, m, SP_B, 1)
            for _ in range(2):
                if ca < NKC:
                    if ca + NS - 1 < NKC:
                        pre.append(ld_h0(ca + NS - 1, 0, 528))
                    c2_chunk(pre.pop(0), ca, 0, 528, SP_A, 0, ca)
                    ca += 1
            if m == 7:
                stat_depth[0] = 1
            if m == 9:
                stats_finish(SP_A)
            if m >= 10:
                for _ in range(3):
                    if cp2 < NKC:
                        pass2(cp2, 0, 528)
                        cp2 += 1
    assert ca == NKC and cp2 == NKC
    stats_finish(SP_B)
    NB6 = 2 * NS
    stgB = [stg[k // 2][:, (k % 2) * 520:(k % 2) * 520 + 512] for k in range(NB6)]
    h1cB = [h1c[k // 2][:, (k % 2) * 520:(k % 2) * 520 + 512] for k in range(NB6)]
    a_half = list(a_toks)
    b_toks = []

    def ld_h0B(m):
        k = m % NB6
        P.dma("sp", stgB[k], h0v[:, m, 528:TC], f"stgB{k}", writes=[buf(f"stgB{k}")], extra=a_half)

    def c2_chunk_B(m):
        k = m % NB6
        sb = buf(f"stgB{k}")
        hb = buf(f"h1cB{k}")
        P.do("dve", lambda e: e.scalar_tensor_tensor(out=h1cB[k], in0=ybuf[:, m, 528:TC], scalar=gv[:, 0, m:m + 1],
                                                     in1=rstd[:, 528:TC], op0=ALU.mult, op1=ALU.mult),
             reads=[ybufs[m], buf("gv"), buf("rstd")], writes=[hb], extra=a_half)
        b_toks.append(P.do("pool" if m % 3 != 2 else "dve", lambda e: e.tensor_tensor(
            out=ybuf[:, m, 528:TC], in0=h1cB[k], in1=stgB[k], op=ALU.add),
            reads=[sb, hb], writes=[ybufs[m]]))
        spill_toks.append(P.dma("sp", h1v[:, m, 528:TC], ybuf[:, m, 528:TC], f"h1st{m % 4}", reads=[ybufs[m]],
                                writes=[h1d[m]], extra=last_spill.get(m % 4, [])))
        last_spill[m % 4] = [spill_toks[-1]]
        stats_add(1, m, ybuf[:, m, :], ybufs[m], SP_B, NKC)

    for m in range(NB6 - 1):
        ld_h0B(m)
    for m in range(NKC):
        if m + NB6 - 1 < NKC:
            ld_h0B(m + NB6 - 1)
        c2_chunk_B(m)
    stats_finish(SP_B)
    for m in range(NKC):
        pass2(m, 528, TC)
    h1nbufs = [buf(f"h1n{q}") for q in range(4)]
    for e_ in ("act", "pool", "dve"):
        P.wait(e_, spill_toks)
    WIN = (2, 4, 8, 16)
    zbufs = [buf(f"z{m}") for m in range(NKC)]
    for gi in range(4):
        w = WIN[gi]
        for hb_ in range(2):
            wiu, wbu = get_w()
            for cl2 in range(2):
                cl = hb_ * 2 + cl2
                par = cl % 2
                ub = buf(f"u{par}")
                res = mm_chunk(wiu, wbu, cl2 * 128, h1n, h1nbufs, CP)
                for (bk, c0, n) in res:
                    P.do("act", lambda e, bk=bk, c0=c0, n=n, par=par: e.activation(
                        out=u_sb[par][:, c0:c0 + n], in_=pb[bk][:, :n], func=AF.Copy),
                        reads=[bank[bk]], writes=[ub])
                cur, curb = u_sb[par], ub
                k = 1
                pp = [s2[par], s4[par]]
                pi_ = 0
                while k < w:
                    dstt = pp[pi_ % 2]
                    db = buf(f"s{pi_ % 2}_{par}")
                    P.do("dve", lambda e, cur=cur, dstt=dstt, k=k: e.tensor_tensor(
                        out=dstt[:, k:TC], in0=cur[:, k:TC], in1=cur[:, 0:TC - k], op=ALU.add),
                        reads=[curb], writes=[db])
                    cur, curb = dstt, db
                    k *= 2
                    pi_ += 1
                P.do("dve", lambda e, cur=cur, cl=cl, w=w, par=par: e.scalar_tensor_tensor(
                    out=mixed[:, cl, :], in0=cur[:, 16:TC], scalar=1.0 / w, in1=u_sb[par][:, 16:TC],
                    op0=ALU.mult, op1=ALU.subtract),
                    reads=[curb, ub], writes=[buf("mixed")])
        for half in range(2):
            gw_ = get_w()
            gr_ = get_w()
            wig, wbg = gw_
            wgi, wbgr = gr_
            for d2 in range(2):
                dl = half * 2 + d2
                m = gi * 4 + dl
                i = m % 2
                res = mm_chunk(wig, wbg, d2 * 128, h1n, h1nbufs, CPO)
                for (bk, c0, n) in res:
                    P.do("act", lambda e, bk=bk, c0=c0, n=n, i=i: e.activation(out=sgc[i][:, c0 - 16:c0 - 16 + n],
                                                                               in_=pb[bk][:, :n], func=AF.Silu),
                         reads=[bank[bk]], writes=[buf(f"sgc{i}")])
            for d2 in range(2):
                dl = half * 2 + d2
                m = gi * 4 + dl
                i = m % 2
                res = mm_chunk(wgi, wbgr, d2 * 128, mixed, [buf("mixed")], PO, nk=4)
                for (bk, c0, n) in res:
                    P.do("dve", lambda e, bk=bk, c0=c0, n=n, m=m, i=i: e.scalar_tensor_tensor(
                        out=zT[:, m, c0:c0 + n], in0=pb[bk][:, :n], scalar=gv[:, 2, m:m + 1],
                        in1=sgc[i][:, c0:c0 + n], op0=ALU.mult, op1=ALU.mult),
                        reads=[bank[bk], buf("gv"), buf(f"sgc{i}")], writes=[zbufs[m]])
    SP_OA = [(0, 512, 6)]
    SP_OB = [(512, 512, 7)]
    out_toks = []
    fin = {"ld": 0}

    def c4_chunk(wi, wbf, ml, m, spieces, stream):
        res = mm_chunk(wi, wbf, ml * 128, zT, zbufs, [(c0, n) for (c0, n, _) in spieces])
        for (bk, c0, n) in res:
            P.do("act", lambda e, bk=bk, c0=c0, n=n, m=m: e.activation(out=ybuf[:, m, c0:c0 + n],
                                                                         in_=pb[bk][:, :n], func=AF.Copy),
                 reads=[bank[bk]], writes=[ybufs[m]])
        stats_add(stream, m, ybuf[:, m, :], ybufs[m], spieces, NKC)

    def ld_h1o(m, lo, hi):
        i = fin["ld"] % NS
        fin["ld"] += 1
        P.dma("sp", stg[i][:, lo:hi], h1v[:, m, 16 + lo:16 + hi], f"stg{i}", reads=[h1d[m]], writes=[buf(f"stg{i}")],
              extra=b_toks + spill_toks)
        return i

    def fin_chunk(i, m, lo, hi, k_add):
        sb = buf(f"stg{i}")
        hb = buf(f"h1c{i}")
        P.do("dve", lambda e: e.scalar_tensor_tensor(out=h1c[i][:, lo:hi], in0=ybuf[:, m, lo:hi],
                                                     scalar=gv[:, 3, m:m + 1], in1=rstd[:, lo:hi],
                                                     op0=ALU.mult, op1=ALU.mult),
             reads=[ybufs[m], buf("gv"), buf("rstd")], writes=[hb], extra=b_toks)
        P.do("pool" if k_add % 3 != 2 else "dve", lambda e: e.tensor_tensor(
            out=h1c[i][:, lo:hi], in0=h1c[i][:, lo:hi], in1=stg[i][:, lo:hi], op=ALU.add),
            reads=[sb, hb], writes=[hb])
        out_toks.append(P.dma("sp", outv[:, m, lo:hi], h1c[i][:, lo:hi], f"ost{i}", reads=[hb]))

    for cb_ in range(8):
        wi, wbf = get_w()
        for ml in range(2):
            c4_chunk(wi, wbf, ml, cb_ * 2 + ml, SP_OA, 0)
    stats_finish(SP_OA)
    pre = [ld_h1o(m, 0, 512) for m in range(NS - 1)]
    for cb_ in range(8):
        wi, wbf = get_w()
        for ml in range(2):
            m = cb_ * 2 + ml
            c4_chunk(wi, wbf, ml, m, SP_OB, 1)
            if m + NS - 1 < NKC:
                pre.append(ld_h1o(m + NS - 1, 0, 512))
            fin_chunk(pre.pop(0), m, 0, 512, m)
    stats_finish(SP_OB)
    pre = [ld_h1o(m, 512, 1024) for m in range(NS - 1)]
    for m in range(NKC):
        if m + NS - 1 < NKC:
            pre.append(ld_h1o(m + NS - 1, 512, 1024))
        fin_chunk(pre.pop(0), m, 512, 1024, m)
    return out_toks


ARENA_BYTES = 206 * 1024


def build(mode):
    nc = bass.Bass("TRN2", target_bir_lowering=False)
    P = Prog(nc)
    ar = Arena(nc, ARENA_BYTES, "arena")
    pb = [nc.alloc_psum_tensor(f"pb{i}", [128, 512], F32) for i in range(8)]

    def din(name, shape, dt=F32):
        return nc.dram_tensor(name, shape, dt, kind="ExternalInput").ap()

    toks = []
    if mode in ("A", "F"):
        ioA = {
            "hT": din("hT", [D, L]), "wA": din("wA", [D, 2048]), "gpre0": din("gpre0", [128, NKC]),
            "cosT": din("cosT", [32, L]), "sinT": din("sinT", [32, L]), "lamv": din("lamv", [128, 4]),
            "sublnG": din("sublnG", [128, 256]), "ident": din("ident", [128, 128]), "perm": din("perm", [32, 32]),
        }
    if mode in ("C", "F"):
        ioC = {
            "hTs": din("hTs", [D, TC]), "wo0": din("wo0", [D, D]), "wi1": din("wi1", [D, 2 * D]),
            "wg": din("wg", [4, 512, 512]), "wo1": din("wo1", [D, D]),
            "gpost0": din("gpost0", [128, NKC]), "gpre1": din("gpre1", [128, NKC]),
            "pscale": din("pscale", [128, NKC]), "gpost1": din("gpost1", [128, NKC]),
        }
        outT = nc.dram_tensor("outT", [D, 1024], F32, kind="ExternalOutput").ap()
        h1_scr = nc.dram_tensor("h1scr", [D, TC], F32).ap()
    if mode == "A":
        ogT_d = nc.dram_tensor("ogT", [9, 512, 512], BF16, kind="ExternalOutput").ap()
        toks = build_A(nc, P, ar, pb, ioA, ogT_d)
    elif mode == "C":
        ogT_in = din("ogTs", [D, TC], BF16)
        toks = build_C(nc, P, ar, pb, ioC, ogT_in, h1_scr, outT, [])
    else:
        cins = [nc.dram_tensor(f"cc_in{t}", [512, 512], BF16) for t in range(9)]
        cin_all = None
        couts = nc.dram_tensor("cc_out", [9, D, 512], BF16)
        ccs = nc.alloc_semaphore(name="ccsem")
        ncc = [0]

        def after_store(t, tok):
            P.wait("pool", [tok])

            def cc(e, t=t):
                e.collective_compute("AllGather", ALU.bypass, replica_groups=[[0, 1, 2, 3], [4, 5, 6, 7]],
                                     ins=[cins[t].ap().opt()], outs=[couts.ap()[t].opt()]).then_inc(ccs)
            P.raw("pool", cc)
            ncc[0] += 1

        ws = WStream(P, ar, ioC)
        ogT_view = ar.alloc([NKC, TC], BF16)
        pre_bytes = ar.off
        ar.reset(0)
        og_pre = {"bufs": [Buf(f"ogT{q}") for q in range(4)], "parts": (0, 1)}
        st = {}

        def ld(e):
            st["r2"] = (e.partition_id() % 4) * 2
        P.raw("sp", ld)
        coutv = couts.ap().rearrange("s (c p) t -> p c s t", p=128)

        def og_src(q4, part):
            if part == 0:
                return coutv[:, 4 * q4:4 * q4 + 4, bass.ds(st["r2"], 1), 496:512]
            return coutv[:, 4 * q4:4 * q4 + 4, bass.ds(st["r2"] + part, 1), :]

        def hook(wa_readers, dead_bytes):
            assert pre_bytes <= dead_bytes, (pre_bytes, dead_bytes)
            ws.issue_upto(ws.PREF, wa_readers)
            tc8 = (ccs, ncc[0])
            for q4 in range(4):
                for part in (0, 1):
                    c_lo, c_hi = (0, 16) if part == 0 else (16, 528)
                    P.dma("sp", ogT_view[:, 4 * q4:4 * q4 + 4, c_lo:c_hi], (lambda q4=q4, part=part: og_src(q4, part)),
                          f"ogin{part}_{q4}", writes=[og_pre["bufs"][q4]], extra=list(wa_readers) + [tc8])

        tA = build_A(nc, P, ar, pb, ioA, cins, after_store, after_last_proj=hook)
        tcc = (ccs, ncc[0])
        for e_ in Prog.ENG:
            P.wait(e_, tA)
        ar.reset(0)
        toks = build_C(nc, P, ar, pb, ioC, og_src, h1_scr, outT, [tcc], ws=ws, og_pre=og_pre)
    P.wait("sp", toks)
    P.emit()
    return nc


def _rope_tables():
    pos = np.arange(L, dtype=np.float32)
    inv = (np.float32(500000.0) ** (-np.arange(0, 32, 2, dtype=np.float32) / np.float32(32))).astype(np.float32)
    ang = (pos[:, None] * inv[None, :]).astype(np.float32)
    c = np.cos(ang).astype(np.float32).T
    s = np.sin(ang).astype(np.float32).T
    cosT = np.concatenate([c, c], 0)
    sinT = np.concatenate([-s, s], 0)
    return np.ascontiguousarray(cosT), np.ascontiguousarray(sinT)


def _pc(v):
    return np.ascontiguousarray(v.reshape(NKC, 128).T)


_CACHE = {}


def _get(mode):
    if mode not in _CACHE:
        _CACHE[mode] = build(mode)
    return _CACHE[mode]


def kernel(x, meta_tokens, pre_norm_g, post_norm_g, attn_w_in, attn_w_out,
           attn_lambda_q1, attn_lambda_k1, attn_lambda_q2, attn_lambda_k2, attn_subln_g,
           pool_w_in, pool_w_group, pool_scale, pool_w_out):
    f = np.float32
    x = np.asarray(x, f)
    B = x.shape[0]
    hT = [np.ascontiguousarray(np.concatenate([np.asarray(meta_tokens, f), x[b]], 0).T) for b in range(B)]
    cosT, sinT = _rope_tables()
    w_in = np.asarray(attn_w_in, f)[0]
    ident = np.eye(128, dtype=f)
    perm = np.zeros((32, 32), f)
    for i in range(32):
        perm[(i + 16) % 32, i] = 1.0
    lamv = np.ascontiguousarray(np.stack([np.asarray(a, f)[0] for a in
                                          (attn_lambda_q1, attn_lambda_k1, attn_lambda_q2, attn_lambda_k2)], 1))
    sublnG = np.ascontiguousarray(np.broadcast_to(np.asarray(attn_subln_g, f)[0][None, :], (128, 256)))
    gpre0 = _pc(np.asarray(pre_norm_g, f)[0])
    cm = {"wo0": np.asarray(attn_w_out, f)[0], "wi1": np.asarray(pool_w_in, f)[0],
          "wg": np.asarray(pool_w_group, f)[0], "wo1": np.asarray(pool_w_out, f)[0],
          "gpost0": _pc(np.asarray(post_norm_g, f)[0]), "gpre1": _pc(np.asarray(pre_norm_g, f)[1]),
          "pscale": _pc(np.asarray(pool_scale, f)[0]), "gpost1": _pc(np.asarray(post_norm_g, f)[1])}
    mapsA = []
    for core in range(8):
        b, r = core // 4, core % 4
        cols = np.concatenate([np.arange(q * 2048 + r * 512, q * 2048 + r * 512 + 512) for q in range(4)])
        mapsA.append({"hT": hT[b], "wA": np.ascontiguousarray(w_in[:, cols]), "gpre0": gpre0, "cosT": cosT,
                      "sinT": sinT, "lamv": lamv, "sublnG": sublnG, "ident": ident, "perm": perm})
    out = np.empty((B, SEQ, D), f)
    if FUSED:
        ncF = _get("F")
        in_maps = []
        for core in range(8):
            b, r = core // 4, core % 4
            d = dict(cm)
            d.update(mapsA[core])
            d["hTs"] = np.ascontiguousarray(hT[b][:, 1024 * r:1024 * r + TC])
            in_maps.append(d)
        res = run_bass_kernel_spmd(ncF, in_maps, core_ids=list(range(8)))
        for core in range(8):
            b, r = core // 4, core % 4
            out[b, 1024 * r:1024 * r + 1024, :] = np.asarray(res.results[core]["outT"]).T
        return out
    resA = run_bass_kernel_spmd(_get("A"), mapsA, core_ids=list(range(8)))
    def _asm(a):
        a = np.asarray(a)
        return np.concatenate([a[0][:, 496:512]] + [a[t_] for t_ in range(1, 9)], 1)
    ogT = [np.concatenate([_asm(resA.results[4 * b + r]["ogT"]) for r in range(4)], 0) for b in range(B)]
    in_maps = []
    for core in range(8):
        b, r = core // 4, core % 4
        d = dict(cm)
        d["hTs"] = np.ascontiguousarray(hT[b][:, 1024 * r:1024 * r + TC])
        d["ogTs"] = np.ascontiguousarray(ogT[b][:, 1024 * r:1024 * r + TC])
        in_maps.append(d)
    resC = run_bass_kernel_spmd(_get("C"), in_maps, core_ids=list(range(8)))
    for core in range(8):
        b, r = core // 4, core % 4
        out[b, 1024 * r:1024 * r + 1024, :] = np.asarray(resC.results[core]["outT"]).T
    return out
```
